# Optimizing a Trainium2 kernel written in Bass

```python
import jax, jax.numpy as jnp
from jax import lax
import numpy as np

D_MODEL = 1024
BATCH = 4
SEQ = 4096
DEPTH = 2

MLA_HEADS = 8
MLA_NOPE = 64
MLA_ROPE = 32
MLA_QK = MLA_NOPE + MLA_ROPE
MLA_V = 64
MLA_Q_RANK = 256
MLA_KV_RANK = 128
ROPE_BASE = 10000.0
ATTN_BLOCK = 128
SG_WIDTH = 512
SG_GROUPS = 8
SG_CHUNK = 128
RW_HEADS = 8
RW_HEAD = 64
RW_WIDTH = RW_HEADS * RW_HEAD
RW_DECAY_RANK = 64
RW_A_RANK = 64
RW_MV_RANK = 32
RW_GATE_RANK = 128
RW_GN_EPS = 64e-5
D_FF = 2816
N_BRANCH = 3
EPS = 1e-6

MLA_COLS = MLA_Q_RANK + MLA_KV_RANK + MLA_ROPE
SG_COLS = 2 * SG_WIDTH
RW_COLS = 3 * RW_WIDTH + RW_DECAY_RANK + RW_A_RANK + RW_GATE_RANK
GATE_COLS = N_BRANCH * D_MODEL
IN_COLS = MLA_COLS + SG_COLS + RW_COLS + GATE_COLS

kernel_name = "hybrid_mla_sgu_rwkv7_block"


def rms_norm(x, g, eps=EPS):
    xf = x.astype(jnp.float32)
    y = xf * lax.rsqrt(jnp.mean(xf * xf, axis=-1, keepdims=True) + eps)
    return (y * g.astype(jnp.float32)).astype(x.dtype)


def swiglu(x, w_gate, w_up, w_down):
    return (jax.nn.silu(x @ w_gate) * (x @ w_up)) @ w_down


def split_cols(p, sizes):
    out, off = [], 0
    for s in sizes:
        out.append(p[..., off:off + s])
        off += s
    return out


def rope_tables(positions):
    inv_freq = ROPE_BASE ** (-jnp.arange(0, MLA_ROPE, 2, dtype=jnp.float32) / MLA_ROPE)
    ang = positions.astype(jnp.float32)[..., None] * inv_freq
    return jnp.cos(ang)[:, :, None, :], jnp.sin(ang)[:, :, None, :]


def apply_rope(x, cos, sin):
    half = x.shape[-1] // 2
    x1, x2 = x[..., :half], x[..., half:]
    c, s = cos.astype(x.dtype), sin.astype(x.dtype)
    return jnp.concatenate([x1 * c - x2 * s, x2 * c + x1 * s], axis=-1)


def causal_block_attention(q, k, v):
    B, S, H, Dq = q.shape
    nb = S // ATTN_BLOCK
    scale = Dq ** -0.5
    qb = q.reshape(B, nb, ATTN_BLOCK, H, Dq).transpose(1, 0, 2, 3, 4)
    kpos = jnp.arange(S)

    def one_block(args):
        q_i, b_i = args
        qpos = b_i * ATTN_BLOCK + jnp.arange(ATTN_BLOCK)
        s = jnp.einsum('bqhd,bkhd->bhqk', q_i, k).astype(jnp.float32) * scale
        s = jnp.where(kpos[None, :] <= qpos[:, None], s, -jnp.inf)
        pr = jax.nn.softmax(s, axis=-1).astype(v.dtype)
        return jnp.einsum('bhqk,bkhd->bqhd', pr, v)

    out = lax.map(one_block, (qb, jnp.arange(nb)))
    return out.transpose(1, 0, 2, 3, 4).reshape(B, S, H, v.shape[-1])


def mla_branch(c_q, c_kv, k_pe, cos, sin, q_a_norm, w_uq, kv_a_norm, w_ukv, q_norm, k_norm):
    B, S, _ = c_q.shape
    q = (rms_norm(c_q, q_a_norm) @ w_uq).reshape(B, S, MLA_HEADS, MLA_QK)
    kv = (rms_norm(c_kv, kv_a_norm) @ w_ukv).reshape(B, S, MLA_HEADS, MLA_NOPE + MLA_V)
    k_nope, v = kv[..., :MLA_NOPE], kv[..., MLA_NOPE:]
    k_rope = jnp.broadcast_to(k_pe[:, :, None, :], (B, S, MLA_HEADS, MLA_ROPE))
    k = jnp.concatenate([k_nope, k_rope], axis=-1)
    q = rms_norm(q, q_norm)
    k = rms_norm(k, k_norm)
    q = jnp.concatenate([q[..., :MLA_NOPE], apply_rope(q[..., MLA_NOPE:], cos, sin)], axis=-1)
    k = jnp.concatenate([k[..., :MLA_NOPE], apply_rope(k[..., MLA_NOPE:], cos, sin)], axis=-1)
    o = causal_block_attention(q, k, v)
    return o.reshape(B, S, MLA_HEADS * MLA_V)


def sgu_branch(p, v_norm, w_s, b_s):
    B, S, _ = p.shape
    p = jax.nn.gelu(p)
    u, v = p[..., :SG_WIDTH], p[..., SG_WIDTH:]
    v = rms_norm(v, v_norm)
    v = v.reshape(B, S // SG_CHUNK, SG_CHUNK, SG_GROUPS, SG_WIDTH // SG_GROUPS)
    mask = jnp.tril(jnp.ones((SG_CHUNK, SG_CHUNK), dtype=bool))
    w = jnp.where(mask[None], w_s, jnp.zeros_like(w_s))
    mixed = jnp.einsum('gts,bnsgc->bntgc', w, v) + b_s.T[None, None, :, :, None]
    return u * mixed.reshape(B, S, SG_WIDTH)


def token_shift(p, mu):
    prev = jnp.pad(p[:, :-1], ((0, 0), (1, 0), (0, 0)))
    return p + (prev - p) * mu


def rwkv7_scan(r, decay, k, v, a, b):
    B, S, H, N = r.shape
    xs = tuple(t.astype(jnp.float32).transpose(1, 0, 2, 3) for t in (r, decay, k, v, a, b))

    def step(state, inp):
        r_t, w_t, k_t, v_t, a_t, b_t = inp
        sa = jnp.einsum('bhvk,bhk->bhv', state, a_t)
        state = (state * w_t[:, :, None, :] + sa[..., None] * b_t[:, :, None, :]
                 + v_t[..., None] * k_t[:, :, None, :])
        return state, jnp.einsum('bhvk,bhk->bhv', state, r_t)

    s0 = jnp.zeros((B, H, N, N), jnp.float32)
    _, ys = lax.scan(step, s0, xs)
    return ys.transpose(1, 0, 2, 3)


def rwkv7_branch(p_rw, mu, w0, w2, a0, a2, g2, k_k, k_a, r_k, ln_g, ln_b, v_first, vres):
    B, S, _ = p_rw.shape
    dt = p_rw.dtype
    p_rw = token_shift(p_rw, mu)
    r, k, v, xw, xa, xg = split_cols(p_rw, [RW_WIDTH, RW_WIDTH, RW_WIDTH,
                                            RW_DECAY_RANK, RW_A_RANK, RW_GATE_RANK])
    w = -jax.nn.softplus(-(w0 + jnp.tanh(xw) @ w2).astype(jnp.float32)) - 0.5
    decay = jnp.exp(-jnp.exp(w))
    a = jax.nn.sigmoid(a0 + xa @ a2)
    g = jax.nn.sigmoid(xg) @ g2
    if vres is None:
        v_first = v
    else:
        v0, v1, v2 = vres
        v = v + (v_first - v) * jax.nn.sigmoid(v0 + (v @ v1) @ v2)
    hs = (B, S, RW_HEADS, RW_HEAD)
    kk = (k * k_k).reshape(hs).astype(jnp.float32)
    kk = kk * lax.rsqrt(jnp.maximum(jnp.sum(kk * kk, axis=-1, keepdims=True), 1e-24))
    k = k * (1.0 + (a - 1.0) * k_a)
    a_h = a.reshape(hs).astype(jnp.float32)
    r_h, k_h, v_h = r.reshape(hs), k.reshape(hs), v.reshape(hs)
    y = rwkv7_scan(r_h, decay.reshape(hs), k_h, v_h, -kk, kk * a_h)
    mean = jnp.mean(y, axis=-1, keepdims=True)
    var = jnp.mean(jnp.square(y - mean), axis=-1, keepdims=True)
    y = ((y - mean) * lax.rsqrt(var + RW_GN_EPS)).reshape(B, S, RW_WIDTH)
    y = (y * ln_g.astype(jnp.float32) + ln_b.astype(jnp.float32)).astype(dt)
    bonus = jnp.sum(r_h * k_h * r_k, axis=-1, keepdims=True) * v_h
    y = (y + bonus.reshape(B, S, RW_WIDTH)) * g
    return y, v_first


def setup_inputs(seed: int = 0) -> dict:
    key = jax.random.key(seed)
    ks = iter(jax.random.split(key, 64))
    f32 = jnp.float32

    def nrm(shape, scale):
        return jax.random.normal(next(ks), shape, f32) * scale

    def gain(shape):
        return 1.0 + 0.02 * jax.random.normal(next(ks), shape, f32)

    L = DEPTH
    x = jax.random.normal(next(ks), (BATCH, SEQ, D_MODEL), f32)
    offs = jax.random.randint(next(ks), (BATCH, 1), 0, 1024, dtype=jnp.int32)
    positions = offs + jnp.arange(SEQ, dtype=jnp.int32)[None, :]
    return {
        "x": x,
        "positions": positions,
        "ffn1_norm": gain((L, D_MODEL)),
        "ffn1_w_gate": nrm((L, D_MODEL, D_FF), D_MODEL ** -0.5),
        "ffn1_w_up": nrm((L, D_MODEL, D_FF), D_MODEL ** -0.5),
        "ffn1_w_down": nrm((L, D_FF, D_MODEL), D_FF ** -0.5),
        "mix_norm": gain((L, D_MODEL)),
        "w_in": nrm((L, D_MODEL, IN_COLS), D_MODEL ** -0.5),
        "mla_q_a_norm": gain((L, MLA_Q_RANK)),
        "mla_w_uq": nrm((L, MLA_Q_RANK, MLA_HEADS * MLA_QK), MLA_Q_RANK ** -0.5),
        "mla_kv_a_norm": gain((L, MLA_KV_RANK)),
        "mla_w_ukv": nrm((L, MLA_KV_RANK, MLA_HEADS * (MLA_NOPE + MLA_V)), MLA_KV_RANK ** -0.5),
        "mla_q_norm": gain((L, MLA_QK)),
        "mla_k_norm": gain((L, MLA_QK)),
        "sg_v_norm": gain((L, SG_WIDTH)),
        "sg_w_s": nrm((L, SG_GROUPS, SG_CHUNK, SG_CHUNK), SG_CHUNK ** -0.5),
        "sg_b_s": gain((L, SG_GROUPS, SG_CHUNK)),
        "rw_mu": jax.random.uniform(next(ks), (L, RW_COLS), f32),
        "rw_w0": jax.random.uniform(next(ks), (L, RW_WIDTH), f32, -6.5, -1.5),
        "rw_w2": nrm((L, RW_DECAY_RANK, RW_WIDTH), 0.1),
        "rw_a0": nrm((L, RW_WIDTH), 0.1),
        "rw_a2": nrm((L, RW_A_RANK, RW_WIDTH), 0.5 * RW_A_RANK ** -0.5),
        "rw_g2": nrm((L, RW_GATE_RANK, RW_WIDTH), RW_GATE_RANK ** -0.5),
        "rw_k_k": 0.85 + 0.02 * jax.random.normal(next(ks), (L, RW_WIDTH), f32),
        "rw_k_a": gain((L, RW_WIDTH)),
        "rw_r_k": nrm((L, RW_HEADS, RW_HEAD), 0.1),
        "rw_ln_g": gain((L, RW_WIDTH)),
        "rw_ln_b": nrm((L, RW_WIDTH), 0.02),
        "rw_v0": gain((L - 1, RW_WIDTH)),
        "rw_v1": nrm((L - 1, RW_WIDTH, RW_MV_RANK), RW_WIDTH ** -0.5),
        "rw_v2": nrm((L - 1, RW_MV_RANK, RW_WIDTH), RW_MV_RANK ** -0.5),
        "w_out_mla": nrm((L, MLA_HEADS * MLA_V, D_MODEL), (MLA_HEADS * MLA_V) ** -0.5),
        "w_out_sg": nrm((L, SG_WIDTH, D_MODEL), SG_WIDTH ** -0.5),
        "w_out_rw": nrm((L, RW_WIDTH, D_MODEL), RW_WIDTH ** -0.5),
        "w_o": nrm((L, D_MODEL, D_MODEL), D_MODEL ** -0.5),
        "ffn2_norm": gain((L, D_MODEL)),
        "ffn2_w_gate": nrm((L, D_MODEL, D_FF), D_MODEL ** -0.5),
        "ffn2_w_up": nrm((L, D_MODEL, D_FF), D_MODEL ** -0.5),
        "ffn2_w_down": nrm((L, D_FF, D_MODEL), D_FF ** -0.5),
    }


def reference(x, positions, ffn1_norm, ffn1_w_gate, ffn1_w_up, ffn1_w_down, mix_norm, w_in,
              mla_q_a_norm, mla_w_uq, mla_kv_a_norm, mla_w_ukv, mla_q_norm, mla_k_norm,
              sg_v_norm, sg_w_s, sg_b_s,
              rw_mu, rw_w0, rw_w2, rw_a0, rw_a2, rw_g2, rw_k_k, rw_k_a, rw_r_k, rw_ln_g, rw_ln_b,
              rw_v0, rw_v1, rw_v2,
              w_out_mla, w_out_sg, w_out_rw, w_o,
              ffn2_norm, ffn2_w_gate, ffn2_w_up, ffn2_w_down):
    B, S, D = x.shape
    cos, sin = rope_tables(positions)
    h = x
    v_first = None
    for i in range(DEPTH):
        h = h + 0.5 * swiglu(rms_norm(h, ffn1_norm[i]), ffn1_w_gate[i], ffn1_w_up[i], ffn1_w_down[i])
        z = rms_norm(h, mix_norm[i])
        p = z @ w_in[i]
        c_q, c_kv, k_pe, p_sg, p_rw, p_gate = split_cols(
            p, [MLA_Q_RANK, MLA_KV_RANK, MLA_ROPE, SG_COLS, RW_COLS, GATE_COLS])
        y_a = mla_branch(c_q, c_kv, k_pe, cos, sin, mla_q_a_norm[i], mla_w_uq[i],
                         mla_kv_a_norm[i], mla_w_ukv[i], mla_q_norm[i], mla_k_norm[i])
        y_b = sgu_branch(p_sg, sg_v_norm[i], sg_w_s[i], sg_b_s[i])
        vres = None if i == 0 else (rw_v0[i - 1], rw_v1[i - 1], rw_v2[i - 1])
        y_c, v_first = rwkv7_branch(p_rw, rw_mu[i], rw_w0[i], rw_w2[i], rw_a0[i], rw_a2[i], rw_g2[i],
                                    rw_k_k[i], rw_k_a[i], rw_r_k[i], rw_ln_g[i], rw_ln_b[i],
                                    v_first, vres)
        gates = jax.nn.sigmoid(p_gate).reshape(B, S, N_BRANCH, D)
        merged = (gates[:, :, 0] * (y_a @ w_out_mla[i])
                  + gates[:, :, 1] * (y_b @ w_out_sg[i])
                  + gates[:, :, 2] * (y_c @ w_out_rw[i]))
        h = h + merged @ w_o[i]
        h = h + 0.5 * swiglu(rms_norm(h, ffn2_norm[i]), ffn2_w_gate[i], ffn2_w_up[i], ffn2_w_down[i])
    return h
```

```python
import math
from contextlib import ExitStack
import numpy as np
import concourse.bass as bass
import concourse.mybir as mybir
from concourse.bass_utils import run_bass_kernel_spmd

F32 = mybir.dt.float32
BF16 = mybir.dt.bfloat16
I32 = mybir.dt.int32
ALU = mybir.AluOpType
AF = mybir.ActivationFunctionType

D = 1024
DFF = 2816
HM = 8
IN_COLS = 6304
EPS = 1e-6
ENG = ("pe", "act", "dve", "pool", "sp")
N_DMA_SEMS = 32
N_SP_SEMS = 24
SB_BYTES = 207 * 1024


class Ev:
    __slots__ = ("kind", "key", "val", "needed")

    def __init__(self, kind, key):
        self.kind, self.key, self.val, self.needed = kind, key, None, False


class Rec:
    __slots__ = ("lo", "hi", "w", "ev", "eng")

    def __init__(self, lo, hi, w, ev, eng):
        self.lo, self.hi, self.w, self.ev, self.eng = lo, hi, w, ev, eng


class Sched:
    def __init__(self, nc):
        self.nc = nc
        self.q = {e: [] for e in ENG}
        self.recs = {}
        self.dma_rr = 0
        self.dma_rr_pool = 0
        self.dma_last = [None] * N_DMA_SEMS
        self.dma_count = [0] * N_DMA_SEMS

    def _deps(self, eng, reads, writes, ev):
        deps = []
        for (key, lo, hi) in reads:
            lst = self.recs.setdefault(key, [])
            for r in lst:
                if r.w and r.lo < hi and lo < r.hi:
                    deps.append(r.ev)
            lst[:] = [r for r in lst if not ((not r.w) and r.eng == eng and r.lo == lo and r.hi == hi
                                             and r.ev.kind == 'eng' and ev.kind == 'eng')]
            lst.append(Rec(lo, hi, False, ev, eng))
        for (key, lo, hi) in writes:
            lst = self.recs.setdefault(key, [])
            for r in lst:
                if r.lo < hi and lo < r.hi:
                    deps.append(r.ev)
            lst[:] = [r for r in lst if not (lo <= r.lo and r.hi <= hi)]
            lst.append(Rec(lo, hi, True, ev, eng))
        out, seen = [], set()
        for d in deps:
            if d is ev or id(d) in seen:
                continue
            seen.add(id(d))
            out.append(d)
        return out

    def op(self, eng, fn, reads=(), writes=()):
        ev = Ev('eng', eng)
        deps = self._deps(eng, reads, writes, ev)
        if eng == 'pe':
            deps = [d for d in deps if not (d.kind == 'eng' and d.key == 'pe')]
        for d in deps:
            d.needed = True
        self.q[eng].append(('op', fn, deps, ev))
        return ev

    def dma(self, queue, out, in_, reads=(), writes=(), **kw):
        if queue == 'pool':
            k = N_SP_SEMS + self.dma_rr_pool
            self.dma_rr_pool = (self.dma_rr_pool + 1) % (N_DMA_SEMS - N_SP_SEMS)
        else:
            k = self.dma_rr
            self.dma_rr = (self.dma_rr + 1) % N_SP_SEMS
        self.dma_count[k] += 1
        ev = Ev('dma', k)
        ev.val = 16 * self.dma_count[k]
        deps = self._deps(queue, reads, writes, ev)
        if self.dma_last[k] is not None:
            deps.append(self.dma_last[k])
        self.dma_last[k] = ev
        for d in deps:
            d.needed = True
        self.q[queue].append(('dma', (out, in_, kw), deps, ev))
        return ev

    def emit(self):
        nc = self.nc
        for e in ENG:
            c = 0
            for item in self.q[e]:
                ev = item[3]
                if ev.kind == 'eng' and ev.needed:
                    c += 1
                    ev.val = c
        with ExitStack() as st:
            esem = {e: st.enter_context(nc.semaphore("s_" + e)) for e in ENG}
            dsem = [st.enter_context(nc.semaphore("d_%d" % i)) for i in range(N_DMA_SEMS)]
            block = st.enter_context(nc.Block())

            def run(engname, engine):
                seen = {}
                for kind, fn, deps, ev in self.q[engname]:
                    for d in deps:
                        key = (d.kind, d.key)
                        if seen.get(key, 0) >= d.val:
                            continue
                        seen[key] = d.val
                        engine.wait_ge(esem[d.key] if d.kind == 'eng' else dsem[d.key], d.val)
                    if kind == 'op':
                        ins = fn(engine)
                        if ev.needed:
                            ins.then_inc(esem[engname], 1)
                    else:
                        out, in_, kw = fn
                        engine.dma_start(out=out, in_=in_, **kw).then_inc(dsem[ev.key], 16)

            @block.tensor
            def _(e):
                run("pe", e)

            @block.scalar
            def _(e):
                run("act", e)

            @block.vector
            def _(e):
                run("dve", e)

            @block.gpsimd
            def _(e):
                run("pool", e)

            @block.sync
            def _(e):
                run("sp", e)
                for k in range(N_DMA_SEMS):
                    if self.dma_count[k]:
                        e.wait_ge(dsem[k], 16 * self.dma_count[k])


def _esz(dt):
    return 2 if dt == BF16 else 4


def reg(ap):
    t = ap.tensor
    row = 1
    for s in list(t.shape)[1:]:
        row *= int(s)
    es = _esz(t.dtype)
    f0 = (int(ap.offset) % row) * es
    ext = 1
    for step, cnt in list(ap.ap)[1:]:
        ext += (int(cnt) - 1) * abs(int(step))
    lo, hi = f0, f0 + ext * es
    if t.name == "psum":
        lo = lo // 2048 * 2048
        hi = (hi + 2047) // 2048 * 2048
    return (t.name, lo, hi)


class Bld:
    def __init__(self, T, n_layers=2, debug=False):
        self.T = T
        self.NG = T // 512
        self.L = n_layers
        self.debug = debug
        nc = self.nc = bass.Bass("TRN2", target_bir_lowering=False)
        self.S = Sched(nc)
        self.inp = {}
        self.scr = {}

    def din(self, name, shape, dt=F32):
        self.inp[name] = self.nc.dram_tensor(name, list(shape), dt, kind="ExternalInput").ap()
        return self.inp[name]

    def dscr(self, name, shape, dt=F32):
        kind = "ExternalOutput" if self.debug else "Internal"
        self.scr[name] = self.nc.dram_tensor(name, list(shape), dt, kind=kind).ap()
        return self.scr[name]

    def reset(self):
        self.ptr = self.persist

    def alloc(self, dt, *free):
        n = 1
        for f in free:
            n *= f
        nb = (n * _esz(dt) + 63) // 64 * 64
        off = self.ptr
        self.ptr += nb
        assert self.ptr <= SB_BYTES, ("SBUF overflow", self.ptr)
        ap = self.big[:, off // 4:(off + nb) // 4]
        if dt != F32:
            ap = ap.bitcast(dt)
        ap = ap[:, 0:n]
        if len(free) == 2:
            ap = ap.rearrange("p (a b) -> p a b", a=free[0])
        elif len(free) == 3:
            ap = ap.rearrange("p (a b c) -> p a b c", a=free[0], b=free[1])
        return ap

    def f32(self, *free):
        return self.alloc(F32, *free)

    def b16(self, *free):
        return self.alloc(BF16, *free)

    def bank(self, i, parts=128, n=512):
        return self.psum[0:parts, i * 512:i * 512 + n]

    def mm(self, out, lhsT, rhs, start=True, stop=True):
        self.S.op('pe', lambda e: e.matmul(out, lhsT=lhsT, rhs=rhs, start=start, stop=stop),
                  reads=[reg(lhsT), reg(rhs)], writes=[reg(out)])

    def tr(self, out, in_, ident):
        self.S.op('pe', lambda e: e.transpose(out=out, in_=in_, identity=ident),
                  reads=[reg(in_), reg(ident)], writes=[reg(out)])

    def act(self, out, in_, func, bias=0.0, scale=1.0):
        rd = [reg(in_)]
        if not isinstance(bias, float):
            rd.append(reg(bias))
        if not isinstance(scale, float):
            rd.append(reg(scale))
        self.S.op('act', lambda e: e.activation(out=out, in_=in_, func=func, bias=bias, scale=scale),
                  reads=rd, writes=[reg(out)])

    def tt(self, eng, out, in0, in1, op):
        self.S.op(eng, lambda e: e.tensor_tensor(out=out, in0=in0, in1=in1, op=op),
                  reads=[reg(in0), reg(in1)], writes=[reg(out)])

    def ts(self, eng, out, in0, s1, op0, s2=None, op1=None):
        rd = [reg(in0)]
        if not isinstance(s1, float):
            rd.append(reg(s1))
        if s2 is not None and not isinstance(s2, float):
            rd.append(reg(s2))
        if op1 is None:
            fn = lambda e: e.tensor_scalar(out=out, in0=in0, scalar1=s1, scalar2=None, op0=op0)
        else:
            fn = lambda e: e.tensor_scalar(out=out, in0=in0, scalar1=s1, scalar2=s2, op0=op0, op1=op1)
        self.S.op(eng, fn, reads=rd, writes=[reg(out)])

    def stt(self, out, in0, scalar, in1, op0, op1):
        rd = [reg(in0), reg(in1)]
        if not isinstance(scalar, float):
            rd.append(reg(scalar))
        self.S.op('dve', lambda e: e.scalar_tensor_tensor(out=out, in0=in0, scalar=scalar, in1=in1, op0=op0, op1=op1),
                  reads=rd, writes=[reg(out)])

    def copy(self, eng, out, in_):
        if eng == 'act':
            self.S.op('act', lambda e: e.copy(out=out, in_=in_), reads=[reg(in_)], writes=[reg(out)])
        else:
            self.S.op(eng, lambda e: e.tensor_copy(out=out, in_=in_), reads=[reg(in_)], writes=[reg(out)])

    def memset(self, eng, out, val):
        self.S.op(eng, lambda e: e.memset(out, val), writes=[reg(out)])

    def recip(self, out, in_):
        self.S.op('dve', lambda e: e.reciprocal(out=out, in_=in_), reads=[reg(in_)], writes=[reg(out)])

    def scan(self, out, d0, d1, init, op0, op1):
        self.S.op('dve', lambda e: e.tensor_tensor_scan(out=out, data0=d0, data1=d1, initial=init, op0=op0, op1=op1),
                  reads=[reg(d0), reg(d1)], writes=[reg(out)])

    def load(self, q, out, in_, key=None, slow=False):
        kw = {"allow_slow_non_contiguous": True} if slow else {}
        self.S.dma(q, out, in_, reads=[key] if key else [], writes=[reg(out)], **kw)

    def store(self, q, out, in_, key):
        self.S.dma(q, out, in_, reads=[reg(in_)], writes=[key])

    def rms_rstd(self, x_tiles, P, nfeat, N, pbank, sqb, sr, rstd, eps=EPS):
        ps = self.bank(pbank, P, N)
        n = len(x_tiles)
        for i, xt in enumerate(x_tiles):
            sq = sqb[i % 2]
            self.act(sq, xt, AF.Square)
            self.mm(ps, self.ones[0:P, 0:P], sq, start=(i == 0), stop=(i == n - 1))
        self.act(sr, ps, AF.Sqrt, bias=self.eps_ap(P, eps), scale=1.0 / nfeat)
        self.recip(rstd, sr)

    def eps_ap(self, P, eps):
        if eps == EPS:
            return self.epsc[0:P, 0:1]
        return self.epsc[0:P, 1:2]


def build(T, n_layers=2, debug=False, phases=None):
    B = Bld(T, n_layers, debug)
    nc, S = B.nc, B.S
    NG = B.NG
    L = 2
    din = B.din
    x_in = din("x", [T, D])
    pos_in = din("positions", [1, T], I32)
    W = {}
    shapes = {
        "ffn1_norm": [L, D], "ffn1_w_gate": [L, D, DFF], "ffn1_w_up": [L, D, DFF], "ffn1_w_down": [L, DFF, D],
        "mix_norm": [L, D], "w_in": [L, D, IN_COLS], "mla_q_a_norm": [L, 256], "mla_w_uq": [L, 256, 768],
        "mla_kv_a_norm": [L, 128], "mla_w_ukv": [L, 128, 1024], "mla_q_norm": [L, 96], "mla_k_norm": [L, 96],
        "sg_v_norm": [L, 512], "sg_w_s": [L, 8, 128, 128], "sg_b_s": [L, 8, 128],
        "rw_mu": [L, 1792], "rw_w0": [L, 512], "rw_w2": [L, 64, 512], "rw_a0": [L, 512], "rw_a2": [L, 64, 512],
        "rw_g2": [L, 128, 512], "rw_k_k": [L, 512], "rw_k_a": [L, 512], "rw_r_k": [L, 8, 64],
        "rw_ln_g": [L, 512], "rw_ln_b": [L, 512], "rw_v0": [1, 512], "rw_v1": [1, 512, 32], "rw_v2": [1, 32, 512],
        "w_out_mla": [L, 512, D], "w_out_sg": [L, 512, D], "w_out_rw": [L, 512, D], "w_o": [L, D, D],
        "ffn2_norm": [L, D], "ffn2_w_gate": [L, D, DFF], "ffn2_w_up": [L, D, DFF], "ffn2_w_down": [L, DFF, D],
    }
    for k, shp in shapes.items():
        W[k] = din(k, shp)
    c_ident = din("c_ident", [128, 128])
    c_m64 = din("c_m64", [64, 4, 64])
    c_amask = din("c_amask", [128, 4, 512])
    c_tril = din("c_tril", [128, 128])
    c_rope = din("c_rope", [96, 4])
    c_esel = din("c_esel", [32, 2, 96])
    c_reset = din("c_reset", [64, 256])
    out_d = nc.dram_tensor("out", [T, D], F32, kind="ExternalOutput").ap()

    hT = B.dscr("hT", [D, T])
    cqn_d = B.dscr("cqn", [256, T], BF16)
    ckvn_d = B.dscr("ckvn", [128, T], BF16)
    kpe_d = B.dscr("kpe", [32, T], BF16)
    prw_d = B.dscr("prw", [1792, T])
    gts_d = B.dscr("gts", [3072, T])
    yaT_d = B.dscr("yaT", [512, T], BF16)
    ybT_d = B.dscr("ybT", [512, T], BF16)
    ycT_d = B.dscr("ycT", [512, T], BF16)
    vf_d = B.dscr("vfirst", [512, T])

    st = ExitStack()
    with st:
        bigt = st.enter_context(nc.sbuf_tensor("big", [128, SB_BYTES // 4], F32))
        B.big = bigt
        pst = st.enter_context(nc.psum_tensor("psum", [128, 4096], F32))
        B.psum = pst
        B.ptr = 0
        B.ident = B.f32(128)
        B.ones = B.f32(128)
        B.epsc = B.f32(4)
        B.load('sp', B.ident, c_ident)
        B.memset('pool', B.ones, 1.0)
        B.memset('pool', B.epsc[:, 0:1], EPS)
        B.memset('pool', B.epsc[:, 1:2], 64e-5)
        B.memset('pool', B.epsc[:, 2:3], 0.0)
        B.memset('pool', B.epsc[:, 3:4], 1e-30)
        B.persist = B.ptr

        def hview(g):
            return hT.rearrange("(dt p) t -> p dt t", p=128)[:, :, g * 512:(g + 1) * 512]

        def phase_x():
            B.reset()
            xt = [B.f32(D), B.f32(D)]
            ho = [B.f32(8, 512), B.f32(8, 512)]
            for g in range(NG):
                hg = ho[g % 2]
                for tt_ in range(4):
                    ti = g * 4 + tt_
                    xb = xt[ti % 2]
                    B.load('sp', xb, x_in[ti * 128:(ti + 1) * 128, :])
                    for half in range(2):
                        pb = B.psum[:, (ti % 2) * 1024 + half * 512:(ti % 2) * 1024 + half * 512 + 512]
                        for j in range(4):
                            dt_ = half * 4 + j
                            B.tr(pb[:, j * 128:(j + 1) * 128], xb[:, dt_ * 128:(dt_ + 1) * 128], B.ident)
                        B.copy('dve' if half == 0 else 'act', hg[:, half * 4:(half + 1) * 4, tt_ * 128:(tt_ + 1) * 128],
                               pb.rearrange("p (a b) -> p a b", a=4))
                B.store('sp', hview(g), hg, ("hT", g * 512, (g + 1) * 512))

        def phase_o():
            B.reset()
            hi = [B.f32(8, 512), B.f32(8, 512)]
            xo = [B.f32(D), B.f32(D)]
            for g in range(NG):
                hg = hi[g % 2]
                B.load('sp', hg, hview(g), ("hT", g * 512, (g + 1) * 512))
                for tt_ in range(4):
                    ti = g * 4 + tt_
                    xb = xo[ti % 2]
                    for half in range(2):
                        pb = B.psum[:, (ti % 2) * 1024 + half * 512:(ti % 2) * 1024 + half * 512 + 512]
                        for j in range(4):
                            dt_ = half * 4 + j
                            B.tr(pb[:, j * 128:(j + 1) * 128], hg[:, dt_, tt_ * 128:(tt_ + 1) * 128], B.ident)
                        B.copy('dve' if half == 0 else 'act', xb[:, half * 512:(half + 1) * 512], pb)
                    B.store('sp', out_d[ti * 128:(ti + 1) * 128, :], xb, ("out", ti, ti + 1))

        def phase_ffn(l, nm):
            B.reset()
            Wg = B.b16(8, DFF)
            Wu = B.b16(8, DFF)
            Wd = B.b16(22, D)
            wgv = W[nm + "_w_gate"][l].rearrange("(kt p) f -> p kt f", p=128)
            wuv = W[nm + "_w_up"][l].rearrange("(kt p) f -> p kt f", p=128)
            wdv = W[nm + "_w_down"][l].rearrange("(kt p) f -> p kt f", p=128)
            for c0, c1 in ((0, 256), (256, 896), (896, 1792), (1792, 2816)):
                B.load('pool', Wg[:, :, c0:c1], wgv[:, :, c0:c1])
                B.load('pool', Wu[:, :, c0:c1], wuv[:, :, c0:c1])
            for k0, k1 in ((0, 6), (6, 14), (14, 22)):
                B.load('pool', Wd[:, k0:k1, :], wdv[:, k0:k1, :])
            gain = B.f32(8)
            B.load('sp', gain, W[nm + "_norm"][l].rearrange("(dt p) -> p dt", p=128), slow=True)
            hb = [B.f32(8, 512), B.f32(8, 512)]
            sqb = [B.f32(512), B.f32(512)]
            sr = sqb[0]
            rstd = B.f32(512)
            xn = B.b16(8, 512)
            actT = B.b16(22, 512)
            sgb = sqb
            for g in range(NG):
                h = hb[g % 2]
                B.load('sp', h, hview(g), ("hT", g * 512, (g + 1) * 512))
                B.rms_rstd([h[:, kt, :] for kt in range(8)], 128, D, 512, 7, sqb, sr, rstd)
                for kt in range(8):
                    B.stt(xn[:, kt, :], h[:, kt, :], gain[:, kt:kt + 1], rstd, ALU.mult, ALU.mult)
                for ft in range(22):
                    pg = B.bank((2 * ft) % 6)
                    pu = B.bank((2 * ft + 1) % 6)
                    for kt in range(8):
                        B.mm(pg, Wg[:, kt, ft * 128:(ft + 1) * 128], xn[:, kt, :], kt == 0, kt == 7)
                    for kt in range(8):
                        B.mm(pu, Wu[:, kt, ft * 128:(ft + 1) * 128], xn[:, kt, :], kt == 0, kt == 7)
                    sg = sgb[ft % 2]
                    B.act(sg, pg, AF.Silu)
                    B.tt('dve', actT[:, ft, :], sg, pu, ALU.mult)
                for dt_ in range(8):
                    pd = B.bank(dt_ % 6)
                    for ft in range(22):
                        B.mm(pd, Wd[:, ft, dt_ * 128:(dt_ + 1) * 128], actT[:, ft, :], ft == 0, ft == 21)
                    B.stt(h[:, dt_, :], pd, 0.5, h[:, dt_, :], ALU.mult, ALU.add)
                B.store('sp', hview(g), h, ("hT", g * 512, (g + 1) * 512))

        B.phase_x, B.phase_o, B.phase_ffn = phase_x, phase_o, phase_ffn
        B.W, B.hT, B.hview = W, hT, hview
        B.c = dict(m64=c_m64, amask=c_amask, tril=c_tril, rope=c_rope, esel=c_esel, reset=c_reset)
        B.d = dict(cqn=cqn_d, ckvn=ckvn_d, kpe=kpe_d, prw=prw_d, gts=gts_d, yaT=yaT_d, ybT=ybT_d, ycT=ycT_d, vf=vf_d,
                   pos=pos_in)

        if phases is None:
            phases = ["x"] + sum([[("ffn", l, "ffn1"), ("proj", l), ("mla", l), ("rwkv", l), ("merge", l),
                                  ("ffn", l, "ffn2")] for l in range(n_layers)], []) + ["o"]
        for ph in phases:
            if ph == "x":
                phase_x()
            elif ph == "o":
                phase_o()
            elif ph[0] == "ffn":
                phase_ffn(ph[1], ph[2])
            elif ph[0] == "proj":
                phase_proj(B, ph[1])
            elif ph[0] == "mla":
                phase_mla(B, ph[1])
            elif ph[0] == "rwkv":
                phase_rwkv(B, ph[1])
            elif ph[0] == "merge":
                phase_merge(B, ph[1])
        S.emit()
    return B


def bcast_rows(ap1d, nparts, n):
    return bass.AP(tensor=ap1d.tensor, offset=ap1d.offset, ap=[[0, nparts], [1, n]])


def phase_proj(B, l):
    T, NG, W = B.T, B.NG, B.W
    B.reset()
    Win = B.b16(8, IN_COLS)
    wv = W["w_in"][l].rearrange("(kt p) f -> p kt f", p=128)
    for c0, c1 in ((0, 416), (1440, 2336), (2336, 3232), (3232, 4768), (4768, 6304), (416, 1440)):
        B.load('pool', Win[:, :, c0:c1], wv[:, :, c0:c1])
    gain = B.f32(8)
    B.load('sp', gain, W["mix_norm"][l].rearrange("(dt p) -> p dt", p=128), slow=True)
    qag = B.f32(2)
    B.load('sp', qag, W["mla_q_a_norm"][l].rearrange("(dt p) -> p dt", p=128), slow=True)
    kvg = B.f32(1)
    B.load('sp', kvg, W["mla_kv_a_norm"][l].rearrange("(dt p) -> p dt", p=128), slow=True)
    wsn = B.f32(8, 128)
    B.load('sp', wsn, W["sg_w_s"][l].rearrange("g t s -> t g s"))
    tril = B.f32(128)
    B.load('sp', tril, B.c["tril"])
    B.tt('pool', wsn, wsn, tril.unsqueeze(1).to_broadcast([128, 8, 128]), ALU.mult)
    WsT = B.b16(8, 128)
    for half in range(2):
        pb = B.bank(4 + half)
        for j in range(4):
            B.tr(pb[:, j * 128:(j + 1) * 128], wsn[:, half * 4 + j, :], B.ident)
        B.copy('dve', WsT[:, half * 4:(half + 1) * 4, :], pb.rearrange("p (a b) -> p a b", a=4))
    sgb_ = B.f32(8)
    B.load('sp', sgb_, W["sg_b_s"][l].rearrange("g t -> t g"), slow=True)
    vgain = B.f32(512)
    B.load('sp', vgain, bcast_rows(W["sg_v_norm"][l], 128, 512))
    h = B.f32(8, 512)
    sqb = [B.f32(512), B.f32(512)]
    sr = B.f32(512)
    rstd = B.f32(512)
    z = B.b16(8, 512)
    gst = [B.f32(4, 512), B.f32(4, 512)]
    rst = [B.f32(4, 512), B.f32(4, 512)]
    cqf = B.f32(2, 512)
    cqb = B.b16(2, 512)
    ckb = B.b16(512)
    kpb = B.b16(512)
    xs = B.f32(512)
    x2 = B.f32(512)
    gu = B.f32(512)
    gv = B.f32(512)
    junk = B.f32(512)
    ss = B.f32(2)
    vn = B.b16(512)
    yb = B.f32(512)
    ybs = B.b16(4, 512)
    prw_v = B.d["prw"].rearrange("(b p) t -> p b t", p=64)
    gts_v = B.d["gts"].rearrange("(j p) t -> p j t", p=128)
    rot = [0]

    def nb():
        rot[0] = (rot[0] + 1) % 4
        return rot[0]

    def gelu(dst, src_ps):
        B.copy('act', xs, src_ps)
        B.act(x2, src_ps, AF.Square)
        B.ts('dve', x2, x2, 0.044715, ALU.mult, 1.0, ALU.add)
        B.tt('dve', x2, x2, xs, ALU.mult)
        B.act(x2, x2, AF.Sigmoid, scale=1.5957691216057308)
        B.tt('dve', dst, xs, x2, ALU.mult)

    for g in range(NG):
        gc = slice(g * 512, (g + 1) * 512)
        B.load('sp', h, B.hview(g), ("hT", g * 512, (g + 1) * 512))
        B.rms_rstd([h[:, kt, :] for kt in range(8)], 128, D, 512, 7, sqb, sr, rstd)
        for kt in range(8):
            B.stt(z[:, kt, :], h[:, kt, :], gain[:, kt:kt + 1], rstd, ALU.mult, ALU.mult)

        def proj(c0, M):
            pb = B.bank(nb(), M)
            for kt in range(8):
                B.mm(pb, Win[:, kt, c0:c0 + M], z[:, kt, :], kt == 0, kt == 7)
            return pb
        for i in range(2):
            pb = proj(i * 128, 128)
            B.copy('act', cqf[:, i, :], pb)
        B.rms_rstd([cqf[:, 0, :], cqf[:, 1, :]], 128, 256, 512, 7, sqb, sr, rstd)
        for i in range(2):
            B.stt(cqb[:, i, :], cqf[:, i, :], qag[:, i:i + 1], rstd, ALU.mult, ALU.mult)
        B.store('sp', B.d["cqn"].rearrange("(i p) t -> p i t", p=128)[:, :, gc], cqb, ("cqn", g * 512, (g + 1) * 512))
        pb = proj(256, 128)
        B.copy('act', cqf[:, 0, :], pb)
        B.rms_rstd([cqf[:, 0, :]], 128, 128, 512, 7, sqb, sr, rstd)
        B.stt(ckb, cqf[:, 0, :], kvg[:, 0:1], rstd, ALU.mult, ALU.mult)
        B.store('sp', B.d["ckvn"][:, gc], ckb, ("ckvn", g * 512, (g + 1) * 512))
        pb = proj(384, 32)
        B.copy('act', kpb[0:32, :], pb)
        B.store('sp', B.d["kpe"][:, gc], kpb[0:32, :], ("kpe", g * 512, (g + 1) * 512))
        for b in range(28):
            pb = proj(1440 + 64 * b, 64)
            stg = rst[(b // 4) % 2]
            B.copy('act' if b % 2 else 'dve', stg[0:64, b % 4, :], pb)
            if b % 4 == 3:
                B.store('sp', prw_v[:, b - 3:b + 1, gc], stg[0:64, :, :], ("prw", g * 512, (g + 1) * 512))
        for j in range(24):
            pb = proj(3232 + 128 * j, 128)
            stg = gst[(j // 4) % 2]
            B.act(stg[:, j % 4, :], pb, AF.Sigmoid)
            if j % 4 == 3:
                B.store('sp', gts_v[:, j - 3:j + 1, gc], stg, ("gts", g * 512, (g + 1) * 512))
        for tt_ in range(4):
            tc_ = slice(tt_ * 128, (tt_ + 1) * 128)
            pu = B.bank(4)
            pv = B.bank(5)
            for kt in range(8):
                B.mm(pu, z[:, kt, tc_], Win[:, kt, 416:928], kt == 0, kt == 7)
            for kt in range(8):
                B.mm(pv, z[:, kt, tc_], Win[:, kt, 928:1440], kt == 0, kt == 7)
            gelu(gu, pu)
            gelu(gv, pv)
            S_ = B.S
            S_.op('act', lambda e: e.activation(out=junk, in_=gv, func=AF.Square, accum_out=ss[:, 0:1]),
                  reads=[reg(gv)], writes=[reg(junk), reg(ss[:, 0:1])])
            B.act(ss[:, 1:2], ss[:, 0:1], AF.Sqrt, bias=B.epsc[:, 0:1], scale=1.0 / 512)
            B.recip(ss[:, 1:2], ss[:, 1:2])
            B.stt(vn, gv, ss[:, 1:2], vgain, ALU.mult, ALU.mult)
            pm = B.bank(6)
            for gi in range(8):
                B.mm(pm[:, gi * 64:(gi + 1) * 64], WsT[:, gi, :], vn[:, gi * 64:(gi + 1) * 64])
            for gi in range(8):
                B.stt(yb[:, gi * 64:(gi + 1) * 64], pm[:, gi * 64:(gi + 1) * 64], sgb_[:, gi:gi + 1],
                      gu[:, gi * 64:(gi + 1) * 64], ALU.add, ALU.mult)
            pt = B.bank(7)
            for j in range(4):
                B.tr(pt[:, j * 128:(j + 1) * 128], yb[:, j * 128:(j + 1) * 128], B.ident)
            B.copy('act', ybs[:, :, tc_], pt.rearrange("p (a b) -> p a b", a=4))
        B.store('sp', B.d["ybT"].rearrange("(j p) t -> p j t", p=128)[:, :, gc], ybs, ("ybT", g * 512, (g + 1) * 512))


def phase_merge(B, l):
    T, NG, W = B.T, B.NG, B.W
    B.reset()
    Wa = B.b16(4, D)
    Wb = B.b16(4, D)
    Wc = B.b16(8, D)
    Wo = B.b16(8, D)
    B.load('pool', Wa, W["w_out_mla"][l].rearrange("(kt p) f -> p kt f", p=128))
    B.load('pool', Wb, W["w_out_sg"][l].rearrange("(kt p) f -> p kt f", p=128))
    B.load('pool', Wc[0:64], W["w_out_rw"][l].rearrange("(h p) f -> p h f", p=64))
    B.load('pool', Wo, W["w_o"][l].rearrange("(kt p) f -> p kt f", p=128))
    h = B.f32(8, 512)
    ya = B.b16(4, 512)
    yb = B.b16(4, 512)
    yc = B.b16(8, 512)
    gl = [B.f32(3, 512), B.f32(3, 512)]
    m = B.b16(8, 512)
    t0 = B.f32(512)
    t1 = B.f32(512)
    t2 = B.f32(512)
    gts_v = B.d["gts"].rearrange("(br dt p) t -> p br dt t", br=3, p=128)
    for g in range(NG):
        gc = slice(g * 512, (g + 1) * 512)
        key = lambda n: (n, g * 512, (g + 1) * 512)
        B.load('sp', h, B.hview(g), key("hT"))
        B.load('sp', ya, B.d["yaT"].rearrange("(j p) t -> p j t", p=128)[:, :, gc], key("yaT"))
        B.load('sp', yb, B.d["ybT"].rearrange("(j p) t -> p j t", p=128)[:, :, gc], key("ybT"))
        B.load('sp', yc[0:64], B.d["ycT"].rearrange("(h p) t -> p h t", p=64)[:, :, gc], key("ycT"))
        for dt_ in range(8):
            dc = slice(dt_ * 128, (dt_ + 1) * 128)
            glt = gl[dt_ % 2]
            B.load('sp', glt, gts_v[:, :, dt_, gc], key("gts"))
            pa = B.bank((3 * dt_) % 6)
            pb = B.bank((3 * dt_ + 1) % 6)
            pc = B.bank((3 * dt_ + 2) % 6)
            for kt in range(4):
                B.mm(pa, Wa[:, kt, dc], ya[:, kt, :], kt == 0, kt == 3)
            for kt in range(4):
                B.mm(pb, Wb[:, kt, dc], yb[:, kt, :], kt == 0, kt == 3)
            for hh in range(8):
                B.mm(pc, Wc[0:64, hh, dc], yc[0:64, hh, :], hh == 0, hh == 7)
            B.tt('dve', t0, pa, glt[:, 0, :], ALU.mult)
            B.tt('dve', t1, pb, glt[:, 1, :], ALU.mult)
            B.tt('dve', t2, pc, glt[:, 2, :], ALU.mult)
            B.tt('pool', t0, t0, t1, ALU.add)
            B.tt('pool', m[:, dt_, :], t0, t2, ALU.add)
        for dt_ in range(8):
            dc = slice(dt_ * 128, (dt_ + 1) * 128)
            po = B.bank(6 + dt_ % 2)
            for kt in range(8):
                B.mm(po, Wo[:, kt, dc], m[:, kt, :], kt == 0, kt == 7)
            B.tt('dve', h[:, dt_, :], h[:, dt_, :], po, ALU.add)
        B.store('sp', B.hview(g), h, key("hT"))


def phase_mla(B, l):
    T, NG, W = B.T, B.NG, B.W
    NT = T // 128
    B.reset()
    cqn = B.b16(2, T)
    ckvn = B.b16(T)
    kpe = B.b16(T)
    B.load('sp', cqn, B.d["cqn"].rearrange("(i p) t -> p i t", p=128), ("cqn", 0, T))
    B.load('sp', ckvn, B.d["ckvn"], ("ckvn", 0, T))
    B.load('sp', kpe[0:32, :], B.d["kpe"], ("kpe", 0, T))
    Ct = B.f32(T)
    Sn = B.f32(T)
    Wq = B.b16(2, 8, 96)
    Wqs = B.b16(2, 8, 96)
    Wk = B.b16(8, 96)
    Wv = B.b16(8, 64)
    esel = B.b16(2, 96)
    wq_v = W["mla_w_uq"][l].rearrange("(kt p) (h d) -> p kt h d", p=128, h=8)
    for kt in range(2):
        B.load('pool', Wq[:, kt], wq_v[:, kt])
        B.load('pool', Wqs[:, kt, :, 0:64], wq_v[:, kt, :, 0:64])
        B.load('pool', Wqs[:, kt, :, 64:80], wq_v[:, kt, :, 80:96])
        B.load('pool', Wqs[:, kt, :, 80:96], wq_v[:, kt, :, 64:80])
    wkv_v = W["mla_w_ukv"][l].rearrange("p (h d) -> p h d", h=8)
    B.memset('pool', Wk, 0.0)
    B.load('pool', Wk[:, :, 0:64], wkv_v[:, :, 0:64])
    B.load('pool', Wv, wkv_v[:, :, 64:128])
    B.load('pool', esel[0:32], B.c["esel"])
    gq = B.f32(2)
    gk = B.f32(2)
    for gt_, nm in ((gq, "mla_q_norm"), (gk, "mla_k_norm")):
        v = W[nm][l].rearrange("(d o) -> d o", o=1)
        B.load('sp', gt_[0:96, 0:1], v)
        B.load('sp', gt_[0:64, 1:2], v[0:64])
        B.load('sp', gt_[64:80, 1:2], v[80:96])
        B.load('sp', gt_[80:96, 1:2], v[64:80])
    ropec = B.f32(4)
    B.load('sp', ropec[0:96], B.c["rope"])
    amask = B.b16(4, 512)
    B.load('pool', amask, B.c["amask"])
    p0 = B.ptr
    posi = B.alloc(I32, T)
    ang = B.f32(T)
    u = B.f32(T)
    ki = B.alloc(I32, T)
    B.ptr = p0
    Vsb = B.b16(NT, 8, 65)
    qT = B.b16(T)
    kT = B.b16(T)
    qa = B.f32(512)
    sq = B.f32(512)
    sr = B.f32(512)
    rstd = B.f32(512)
    t1 = B.f32(512)
    t2 = B.f32(512)
    PT = [B.b16(512) for _ in range(3)]
    oT = B.f32(512)
    rec = B.f32(512)
    yas = [B.b16(512), B.b16(512)]
    P96 = slice(0, 96)
    B.load('sp', posi[P96], bcast_rows(B.d["pos"][0], 96, T))
    B.copy('dve', u[P96], posi[P96])
    B.ts('dve', ang[P96], u[P96], ropec[P96, 0:1], ALU.mult)
    TWO_PI = 2.0 * math.pi
    C1 = 6.28125
    C2 = TWO_PI - C1
    for tab, shift in ((Ct, math.pi / 2), (Sn, 0.0)):
        B.ts('dve', u[P96], ang[P96], float(shift), ALU.add, float(1.0 / TWO_PI), ALU.mult)
        B.copy('dve', ki[P96], u[P96])
        B.copy('dve', u[P96], ki[P96])
        B.stt(tab[P96], u[P96], -C1, ang[P96], ALU.mult, ALU.add)
        B.stt(tab[P96], u[P96], -C2, tab[P96], ALU.mult, ALU.add)
        if shift:
            B.ts('dve', tab[P96], tab[P96], float(shift), ALU.add)
        B.ts('dve', tab[P96], tab[P96], 3.1415925, ALU.min, -3.1415925, ALU.max)
        B.act(tab[P96], tab[P96], AF.Sin)
    B.ts('dve', Sn[P96], Sn[P96], ropec[P96, 1:2], ALU.mult)
    B.memset('pool', Vsb[:, :, :, 64:65], 1.0)
    for tt_ in range(NT):
        pv = B.bank(4 + tt_ % 2)
        B.mm(pv, ckvn[:, tt_ * 128:(tt_ + 1) * 128], Wv.rearrange("p h d -> p (h d)"))
        B.copy('act' if tt_ % 2 else 'dve', Vsb[:, tt_, :, 0:64], pv.rearrange("p (h d) -> p h d", h=8))
    scale = 96.0 ** -0.5

    tmp2 = [B.f32(512) for _ in range(6)]
    tmps = [(qa, sq, sr, rstd, t1, t2), tuple(tmp2)]

    def qk_stages(h, g, is_q):
        gc = slice(g * 512, (g + 1) * 512)
        qa_, sq_, sr_, rstd_, t1_, t2_ = tmps[0 if is_q else 1]
        if is_q:
            pa, pb, p6 = B.bank(4, 96), B.bank(5, 96), B.bank(6, 96)
            gn, dst = gq, qT
        else:
            pa, pb, p6 = B.bank(2, 96), B.bank(3, 96), B.bank(7, 96)
            gn, dst = gk, kT

        def s0():
            if is_q:
                for pp, ww in ((pa, Wq), (pb, Wqs)):
                    for kt in range(2):
                        B.mm(pp, ww[:, kt, h, :], cqn[:, kt, gc], kt == 0, kt == 1)
            else:
                for pp, si in ((pa, 0), (pb, 1)):
                    B.mm(pp, Wk[:, h, :], ckvn[:, gc], True, False)
                    B.mm(pp, esel[0:32, si, :], kpe[0:32, gc], False, True)

        def s1():
            B.act(sq_[P96], pa, AF.Square)
            B.copy('act', qa_[P96], pa)

        def s2():
            B.mm(p6, B.ones[0:96, 0:96], sq_[P96])
            B.stt(t1_[P96], qa_[P96], gn[P96, 0:1], Ct[P96, gc], ALU.mult, ALU.mult)
            B.stt(t2_[P96], pb, gn[P96, 1:2], Sn[P96, gc], ALU.mult, ALU.mult)

        def s3():
            B.act(sr_[P96], p6, AF.Sqrt, bias=B.epsc[0:96, 0:1], scale=1.0 / 96)
            B.tt('pool', t1_[P96], t1_[P96], t2_[P96], ALU.add)

        def s4():
            B.recip(rstd_[P96], sr_[P96])
            B.tt('dve', dst[P96, gc], t1_[P96], rstd_[P96], ALU.mult)
        return [s0, s1, s2, s3, s4]

    for h in range(8):
        for g in range(NG):
            for fq, fk in zip(qk_stages(h, g, True), qk_stages(h, g, False)):
                fq()
                fk()
        for g in range(NG):
            gc = slice(g * 512, (g + 1) * 512)
            po = B.bank(2 + g % 2, 65)
            nk = 4 * (g + 1)

            def s_mm(kt):
                B.mm(B.bank(kt % 2), kT[P96, kt * 128:(kt + 1) * 128], qT[P96, gc])
            s_mm(0)
            for kt in range(nk):
                if kt + 1 < nk:
                    s_mm(kt + 1)
                pt = PT[kt % 3]
                B.act(pt, B.bank(kt % 2), AF.Exp, scale=float(scale))
                if kt >= 4 * g:
                    B.tt('pool', pt, pt, amask[:, kt - 4 * g, :], ALU.mult)
                B.mm(po, Vsb[:, kt, h, :], pt, kt == 0, kt == nk - 1)
            B.copy('act', oT[0:65], po)
            B.recip(rec[64:65], oT[64:65])
            p7 = B.bank(7, 64)
            B.mm(p7, B.ones[64:65, 0:64], rec[64:65])
            ys = yas[g % 2]
            B.tt('dve', ys[0:64], oT[0:64], p7, ALU.mult)
            B.store('sp', B.d["yaT"][h * 64:(h + 1) * 64, gc], ys[0:64], ("yaT", g * 512, (g + 1) * 512))


def phase_rwkv(B, l):
    T, W = B.T, B.W
    GR = 128
    NGR = T // GR
    E05 = math.exp(-0.5)
    P = slice(0, 64)
    B.reset()

    def v64(name, pat, **kw):
        t = B.f32(*kw.pop("shape"))
        B.load('sp', t[P], W[name][kw.pop("li", l)].rearrange(pat, **kw), slow=True)
        return t
    mu = v64("rw_mu", "(b p) -> p b", shape=(28,), p=64)
    w0 = v64("rw_w0", "(h p) -> p h", shape=(8,), p=64)
    a0 = v64("rw_a0", "(h p) -> p h", shape=(8,), p=64)
    kkc = v64("rw_k_k", "(h p) -> p h", shape=(8,), p=64)
    kac = v64("rw_k_a", "(h p) -> p h", shape=(8,), p=64)
    lng = v64("rw_ln_g", "(h p) -> p h", shape=(8,), p=64)
    lnb = v64("rw_ln_b", "(h p) -> p h", shape=(8,), p=64)
    rkc = v64("rw_r_k", "h k -> k h", shape=(8,))
    w2 = B.b16(512)
    a2 = B.b16(512)
    g2 = B.b16(2, 512)
    B.load('pool', w2[P], W["rw_w2"][l])
    B.load('pool', a2[P], W["rw_a2"][l])
    B.load('pool', g2[P], W["rw_g2"][l].rearrange("(two p) f -> p two f", p=64))
    if l > 0:
        v0 = v64("rw_v0", "(h p) -> p h", shape=(8,), p=64, li=0)
        v1 = B.f32(8, 32)
        v2 = B.f32(512)
        B.load('sp', v1[P], W["rw_v1"][0].rearrange("(h p) r -> p h r", p=64))
        B.load('sp', v2[0:32], W["rw_v2"][0])
    m64 = B.f32(4, 64)
    B.load('sp', m64[P], B.c["m64"])
    rmask = B.f32(GR)
    B.load("sp", rmask[P], B.c["reset"][:, 0:GR])
    onesm = B.f32(64)
    B.memset('pool', onesm[P], 1.0 / 64)
    identb = B.b16(64)
    B.copy('dve', identb[P], B.ident[P, 0:64])
    ST = B.f32(8, 64)
    STb = B.b16(8, 64)
    STb2 = B.b16(8, 64)
    B.memset('pool', ST[P], 0.0)
    B.memset('pool', STb[P], 0.0)
    B.memset('pool', STb2[P], 0.0)
    pr = B.f32(28, GR + 1)
    sig = B.f32(8, GR)
    aa = B.f32(8, GR)
    kkn = B.f32(8, GR)
    cs = B.f32(8, GR)
    X1 = B.f32(8, GR)
    X2 = B.f32(8, GR)
    dsh = B.f32(28, GR)
    STg = B.f32(8, 64)
    sx = B.b16(2, GR)
    txw = B.b16(GR)
    xab = B.b16(GR)
    vr = B.f32(GR)
    T1 = B.f32(8, GR)
    T2 = B.f32(8, GR)
    T3 = B.f32(8, GR)
    DB = []
    for _ in range(2):
        DB.append(dict(At=B.b16(8, GR), Bt=B.b16(8, GR), Kt=B.b16(8, GR), Rt=B.b16(8, GR), gam=B.f32(8, GR),
                       psh=B.f32(28, GR), gt=B.f32(8, GR), yT=B.f32(8, GR), yo=B.b16(8, GR)))
    bnames = ["NbaT", "Nba", "Nka", "Mbr", "Mkr", "Btok", "Ktok", "Vtok", "Pa", "PTa", "Ta", "TTa", "Pb", "PTb", "Tb", "TTb",
              "X0", "U"]
    fnames = ["Sf"]
    MM = []
    for _ in range(2):
        d = {n: B.b16(8, 64) for n in bnames}
        d.update({n: B.f32(8, 64) for n in fnames})
        MM.append(d)
    rot = [0]
    evr = [0]

    def nb():
        rot[0] = (rot[0] + 1) % 8
        return B.bank(rot[0], 64)

    def ev_eng():
        evr[0] ^= 1
        return 'act' if evr[0] else 'dve'

    def bv(b):
        return b.rearrange("p (h c) -> p h c", h=8)

    def mbc(i):
        return m64[P, i:i + 1, :].to_broadcast([64, 8, 64])

    prw_v = B.d["prw"].rearrange("(b p) t -> p b t", p=64)
    vf_v = B.d["vf"].rearrange("(h p) t -> p h t", p=64)
    yc_v = B.d["ycT"].rearrange("(h p) t -> p h t", p=64)

    def bc8(t):
        return t[P, 0:8].unsqueeze(2).to_broadcast([64, 8, GR])

    def mm8(lf, rf, evac):
        pb = nb()
        for h in range(8):
            B.mm(pb[:, h * 64:(h + 1) * 64], lf(h), rf(h))
        evac(bv(pb))

    def cp(dst):
        return lambda pv: B.copy(ev_eng(), dst, pv)

    def half(ap2, i):
        return ap2

    def prep_steps(g):
        G = DB[g % 2]
        psh, At, Bt, Kt, Rt, gam, gt = G["psh"], G["At"], G["Bt"], G["Kt"], G["Rt"], G["gam"], G["gt"]
        g0 = g * GR
        key = lambda n: (n, g0, g0 + GR)
        r_ = psh[P, 0:8, :]
        k_ = psh[P, 8:16, :]
        v_ = psh[P, 16:24, :]
        st = []

        def ld():
            if g == 0:
                B.memset('pool', pr[P, :, 0:1], 0.0)
                B.load('sp', pr[P, :, 1:GR + 1], prw_v[:, :, 0:GR], ("prw", 0, GR))
            else:
                B.load('sp', pr[P, :, :], prw_v[:, :, g0 - 1:g0 + GR], ("prw", g0 - 1, g0 + GR))
            if l > 0:
                B.load('sp', T3[P], vf_v[:, :, g0:g0 + GR], key("vfirst"))
        st.append(ld)
        def shift():
            B.tt('pool', dsh[P], pr[P, :, 0:GR], pr[P, :, 1:GR + 1], ALU.subtract)
            B.tt('pool', dsh[P], dsh[P], mu[P, 0:28].unsqueeze(2).to_broadcast([64, 28, GR]), ALU.mult)
            B.tt('dve', psh[P], dsh[P], pr[P, :, 1:GR + 1], ALU.add)
        st.append(shift)

        def lora0():
            B.act(txw[P], psh[P, 24, :], AF.Tanh)
            B.act(sx[P], psh[P, 26:28, :], AF.Sigmoid)
            B.copy('dve', xab[P], psh[P, 25, :])
        st.append(lora0)
        for h0 in range(0, 8, 2):
            def lora(h0=h0):
                for h in (h0, h0 + 1):
                    hc = slice(h * 64, (h + 1) * 64)
                    pb = nb()
                    B.mm(pb[:, 0:GR], w2[P, hc], txw[P])
                    B.act(sig[P, h, :], pb[:, 0:GR], AF.Sigmoid, bias=w0[P, h:h + 1])
                    pb = nb()
                    B.mm(pb[:, 0:GR], a2[P, hc], xab[P])
                    B.act(aa[P, h, :], pb[:, 0:GR], AF.Sigmoid, bias=a0[P, h:h + 1])
                    pb = nb()
                    B.mm(pb[:, 0:GR], g2[P, 0, hc], sx[P, 0, :], True, False)
                    B.mm(pb[:, 0:GR], g2[P, 1, hc], sx[P, 1, :], False, True)
                    B.copy('act', gt[P, h, :], pb[:, 0:GR])
            st.append(lora)
        if l == 0:
            st.append(lambda: B.store('sp', vf_v[:, :, g0:g0 + GR], v_, key("vfirst")))
        else:
            def vres0():
                pb = nb()
                for h in range(8):
                    B.mm(pb[0:32, 0:GR], v1[P, h, :], v_[:, h, :], h == 0, h == 7)
                B.copy('act', vr[0:32], pb[0:32, 0:GR])
            st.append(vres0)

            def vres1():
                for hq in range(2):
                    pb = nb()
                    for j in range(4):
                        h = hq * 4 + j
                        B.mm(pb[:, j * GR:(j + 1) * GR], v2[0:32, h * 64:(h + 1) * 64], vr[0:32])
                    for j in range(4):
                        h = hq * 4 + j
                        B.act(X2[P, h, :], pb[:, j * GR:(j + 1) * GR], AF.Sigmoid, bias=v0[P, h:h + 1])
            st.append(vres1)

            def vres2():
                B.tt('pool', T3[P], T3[P], v_, ALU.subtract)
                B.tt('pool', T3[P], T3[P], X2[P], ALU.mult)
                B.tt('dve', v_, v_, T3[P], ALU.add)
            st.append(vres2)

        def kk0():
            B.tt('dve', kkn[P], k_, bc8(kkc), ALU.mult)
            B.tt('pool', X1[P], kkn[P], kkn[P], ALU.mult)
        st.append(kk0)

        def kk1():
            for i in range(2):
                pb = nb()
                B.mm(pb, B.ones[P, 0:64], X1[P, 4 * i:4 * i + 4, :])
                B.act(X2[P, 4 * i:4 * i + 4, :], pb.rearrange("p (a b) -> p a b", a=4), AF.Sqrt, bias=B.epsc[P, 3:4])
            B.recip(X2[P], X2[P])
            B.tt('dve', kkn[P], kkn[P], X2[P], ALU.mult)
        st.append(kk1)

        def kmod():
            B.ts('dve', X1[P], aa[P], -1.0, ALU.add)
            B.tt('pool', X1[P], X1[P], bc8(kac), ALU.mult)
            B.stt(k_, X1[P], 1.0, k_, ALU.add, ALU.mult)
        st.append(kmod)

        def dec0():
            for h in range(8):
                B.scan(cs[P, h, :], rmask[P, 0:GR], sig[P, h, :], 0.0, ALU.mult, ALU.add)
            B.act(gam[P], cs[P], AF.Exp, scale=-E05)
            B.act(X2[P], cs[P], AF.Exp, scale=E05)
            B.tt('pool', X1[P], cs[P], sig[P], ALU.subtract)
            B.act(X1[P], X1[P], AF.Exp, scale=-E05)
        st.append(dec0)

        def dec1():
            B.stt(At[P], X1[P], -1.0, kkn[P], ALU.mult, ALU.mult)
            B.tt('pool', X1[P], kkn[P], aa[P], ALU.mult)
            B.tt('dve', Bt[P], X1[P], X2[P], ALU.mult)
            B.tt('dve', Kt[P], k_, X2[P], ALU.mult)
            B.tt('pool', Rt[P], r_, gam[P], ALU.mult)
        st.append(dec1)
        return st

    def chunk_stages(g, c, M):
        G = DB[g % 2]
        psh, At, Bt, Kt, Rt, gam, yT = G["psh"], G["At"], G["Bt"], G["Kt"], G["Rt"], G["gam"], G["yT"]
        cc = slice(c * 64, (c + 1) * 64)
        STo, STn = (STb, STb2) if c == 0 else (STb2, STb)
        st = []
        for nm, X, Y, mi in (("NbaT", At, Bt, 2), ("Nba", Bt, At, 0), ("Nka", Kt, At, 0), ("Mbr", Bt, Rt, 1),
                             ("Mkr", Kt, Rt, 1)):
            if True:
                st.append(lambda nm=nm, X=X, Y=Y, mi=mi: mm8(lambda h: X[P, h, cc], lambda h: Y[P, h, cc],
                                                             lambda pv: B.tt('dve', M[nm][P], pv, mbc(mi), ALU.mult)))
            else:
                def masked(pv, nm=nm, mi=mi):
                    B.copy('act', M[nm][P], pv)
                    B.tt('pool', M[nm][P], M[nm][P], mbc(mi), ALU.mult)
                st.append(lambda nm=nm, X=X, Y=Y, masked=masked: mm8(lambda h: X[P, h, cc], lambda h: Y[P, h, cc], masked))
        for nm, X in (("Btok", Bt), ("Ktok", Kt)):
            st.append(lambda nm=nm, X=X: mm8(lambda h: X[P, h, cc], lambda h: identb[P, :],
                                             lambda pv: B.copy('act', M[nm][P], pv)))

        def vtok():
            pb = nb()
            for h in range(8):
                B.tr(pb[:, h * 64:(h + 1) * 64], psh[P, 16 + h, cc], B.ident[P, 0:64])
            B.copy('act', M["Vtok"][P], bv(pb))
        st.append(vtok)

        def tinit():
            B.tt('dve', M["Ta"][P], M["Nba"][P], mbc(3), ALU.add)
            B.tt('dve', M["TTa"][P], M["NbaT"][P], mbc(3), ALU.add)
        st.append(tinit)
        sqs, tss = [], []
        cur = [M["Nba"], M["NbaT"], M["Ta"], M["TTa"]]
        sets = [(M["Pa"], M["PTa"], M["Tb"], M["TTb"]), (M["Pb"], M["PTb"], M["Ta"], M["TTa"])]
        for s_ in range(5):
            Pn, PTn, Tn, TTn = sets[s_ % 2]
            Pc, PTc, Tc, TTc = cur
            last = (s_ == 4)

            def sq_stage(Pc=Pc, PTc=PTc, Pn=Pn, PTn=PTn, last=last):
                mm8(lambda h: PTc[P, h, :], lambda h: Pc[P, h, :], lambda pv: B.copy('act', Pn[P], pv))
                if not last:
                    mm8(lambda h: Pc[P, h, :], lambda h: PTc[P, h, :], lambda pv: B.copy('dve', PTn[P], pv))

            def t_stage(Pn=Pn, Tc=Tc, TTc=TTc, Tn=Tn, TTn=TTn, last=last):
                pb = nb()
                for h in range(8):
                    o_ = pb[:, h * 64:(h + 1) * 64]
                    B.mm(o_, TTc[P, h, :], Pn[P, h, :], True, False)
                    B.mm(o_, identb[P, :], Tc[P, h, :], False, True)
                B.copy('dve', Tn[P], bv(pb))
                if not last:
                    pb = nb()
                    for h in range(8):
                        o_ = pb[:, h * 64:(h + 1) * 64]
                        B.mm(o_, Pn[P, h, :], TTc[P, h, :], True, False)
                        B.mm(o_, identb[P, :], TTc[P, h, :], False, True)
                    B.copy('act', TTn[P], bv(pb))
            sqs.append(sq_stage)
            tss.append(t_stage)
            cur = [Pn, PTn, Tn, TTn]
        Tfin = cur[2]
        st.append(sqs[0])
        for i_ in range(1, 5):
            st.append(lambda i_=i_: (sqs[i_](), tss[i_ - 1]()))
        st.append(tss[4])
        gbc = gam[P, :, c * 64 + 63:c * 64 + 64].to_broadcast([64, 8, 64])

        def seq1():
            B.tt('pool', STg[P], ST[P], gbc, ALU.mult)
            pb = nb()
            for h in range(8):
                o_ = pb[:, h * 64:(h + 1) * 64]
                B.mm(o_, M["Nka"][P, h, :], M["Vtok"][P, h, :], True, False)
                B.mm(o_, At[P, h, cc], STo[P, h, :], False, True)
            B.copy('act', M["X0"][P], bv(pb))

        def seq2():
            mm8(lambda h: Tfin[P, h, :], lambda h: M["X0"][P, h, :], lambda pv: B.copy('act', M["U"][P], pv))

        def seq3():
            pb = nb()
            for h in range(8):
                o_ = pb[:, h * 64:(h + 1) * 64]
                B.mm(o_, M["Ktok"][P, h, :], M["Vtok"][P, h, :], True, False)
                B.mm(o_, M["Btok"][P, h, :], M["U"][P, h, :], False, True)
            for h in range(8):
                B.stt(STn[P, h, :], pb[:, h * 64:(h + 1) * 64], gam[P, h, c * 64 + 63:c * 64 + 64], STg[P, h, :],
                      ALU.mult, ALU.add)
            pb2 = nb()
            for h in range(8):
                o_ = pb2[:, h * 64:(h + 1) * 64]
                B.mm(o_, M["Vtok"][P, h, :], M["Mkr"][P, h, :], True, False)
                B.mm(o_, STo[P, h, :], Rt[P, h, cc], False, False)
                B.mm(o_, M["U"][P, h, :], M["Mbr"][P, h, :], False, True)
            B.tt('dve', M["Sf"][P], bv(pb), gbc, ALU.mult)
            B.tt('pool', ST[P], M["Sf"][P], STg[P], ALU.add)
            B.copy('act', yT[P, :, cc], bv(pb2))
        return st, [seq1, seq2, seq3]

    def tail_steps(g):
        G = DB[g % 2]
        psh, gt, yT, yo = G["psh"], G["gt"], G["yT"], G["yo"]
        g0 = g * GR
        r_ = psh[P, 0:8, :]
        k_ = psh[P, 8:16, :]
        st = []

        def t0():
            for i in range(2):
                hs = slice(4 * i, 4 * i + 4)
                pb = nb()
                B.mm(pb, onesm[P, 0:64], yT[P, hs, :])
                B.tt('dve', T1[P, hs, :], yT[P, hs, :], pb.rearrange("p (a b) -> p a b", a=4), ALU.subtract)
            B.tt('pool', T2[P], T1[P], T1[P], ALU.mult)
        st.append(t0)

        def t1():
            for i in range(2):
                hs = slice(4 * i, 4 * i + 4)
                pb = nb()
                B.mm(pb, onesm[P, 0:64], T2[P, hs, :])
                B.act(T2[P, hs, :], pb.rearrange("p (a b) -> p a b", a=4), AF.Sqrt, bias=B.epsc[P, 1:2])
            B.recip(T2[P], T2[P])
            B.tt('dve', T1[P], T1[P], T2[P], ALU.mult)
        st.append(t1)

        def t2():
            B.tt('pool', T1[P], T1[P], bc8(lng), ALU.mult)
            B.tt('pool', T1[P], T1[P], bc8(lnb), ALU.add)
            B.tt('pool', T2[P], r_, k_, ALU.mult)
            B.tt('pool', T2[P], T2[P], bc8(rkc), ALU.mult)
        st.append(t2)

        def t3():
            for i in range(2):
                hs = slice(4 * i, 4 * i + 4)
                pb = nb()
                B.mm(pb, B.ones[P, 0:64], T2[P, hs, :])
                B.tt('dve', T2[P, hs, :], pb.rearrange("p (a b) -> p a b", a=4), psh[P, 16 + 4 * i:20 + 4 * i, :], ALU.mult)
            B.tt('pool', T1[P], T1[P], T2[P], ALU.add)
            B.tt('dve', yo[P], T1[P], gt[P], ALU.mult)
            B.store('sp', yc_v[:, :, g0:g0 + GR], yo[P], ("ycT", g0, g0 + GR))
        st.append(t3)
        return st

    for f in prep_steps(0):
        f()
    for g in range(NGR):
        sa, seqa = chunk_stages(g, 0, MM[0])
        sb_, seqb = chunk_stages(g, 1, MM[1])
        main = []
        for fa, fb in zip(sa, sb_):
            main.append((fa, fb))
        for f in seqa:
            main.append((f,))
        for f in seqb:
            main.append((f,))
        side = []
        if g > 0:
            side += tail_steps(g - 1)
        if g + 1 < NGR:
            side += prep_steps(g + 1)
        ns, nm_ = len(side), len(main)
        si = 0
        for i, fs in enumerate(main):
            for f in fs:
                f()
            tgt = (i + 1) * ns // nm_
            while si < tgt:
                side[si]()
                si += 1
    for f in tail_steps(NGR - 1):
        f()


def make_consts():
    c = {}
    c["c_ident"] = np.eye(128, dtype=np.float32)
    m = np.zeros((64, 4, 64), np.float32)
    r = np.arange(64)[:, None]
    cc = np.arange(64)[None, :]
    m[:, 0, :] = (r < cc)
    m[:, 1, :] = (r <= cc)
    m[:, 2, :] = (r > cc)
    m[:, 3, :] = (r == cc)
    c["c_m64"] = m
    am = np.zeros((128, 4, 512), np.float32)
    k = np.arange(128)[:, None]
    q = np.arange(512)[None, :]
    for j in range(4):
        am[:, j, :] = (q >= j * 128 + k)
    c["c_amask"] = am
    t = np.arange(128)[:, None]
    s = np.arange(128)[None, :]
    c["c_tril"] = (t >= s).astype(np.float32)
    rope = np.zeros((96, 4), np.float32)
    inv = (10000.0 ** (-np.arange(0, 32, 2, dtype=np.float32) / 32)).astype(np.float32)
    rope[64:80, 0] = inv
    rope[80:96, 0] = inv
    rope[64:80, 1] = -1.0
    rope[80:96, 1] = 1.0
    c["c_rope"] = rope
    es = np.zeros((32, 2, 96), np.float32)
    for i in range(32):
        es[i, 0, 64 + i] = 1.0
        es[i, 1, 64 + (i + 16) % 32] = 1.0
    c["c_esel"] = es
    rs = np.ones((64, 256), np.float32)
    rs[:, ::64] = 0.0
    c["c_reset"] = rs
    return c


_CACHE = {}


def kernel(**inputs):
    T = 4096
    if "nc" not in _CACHE:
        _CACHE["nc"] = build(T).nc
    nc = _CACHE["nc"]
    consts = make_consts()
    x = np.ascontiguousarray(inputs["x"], dtype=np.float32)
    pos = np.ascontiguousarray(inputs["positions"], dtype=np.int32)
    in_maps = []
    active = {0: 0, 1: 1, 4: 2, 5: 3}
    wts = {k: np.ascontiguousarray(v, dtype=np.float32) for k, v in inputs.items() if k not in ("x", "positions")}
    zw = {k: np.zeros_like(v) for k, v in wts.items()}
    for c in range(8):
        if c in active:
            b = active[c]
            m = {"x": x[b], "positions": pos[b:b + 1]}
            m.update(wts)
        else:
            m = {"x": np.zeros_like(x[0]), "positions": np.zeros_like(pos[0:1])}
            m.update(zw)
        m.update(consts)
        in_maps.append(m)
    res = run_bass_kernel_spmd(nc, in_maps, core_ids=list(range(8)))
    out = np.stack([res.results[c]["out"] for c in (0, 1, 4, 5)], axis=0)
    return out.astype(np.float32)
```

```python
import math
from contextlib import ExitStack
import numpy as np
import concourse.bass as bass
import concourse.mybir as mybir
from concourse.bass_utils import run_bass_kernel_spmd

F32 = mybir.dt.float32
BF16 = mybir.dt.bfloat16
I32 = mybir.dt.int32
ALU = mybir.AluOpType
AF = mybir.ActivationFunctionType

D = 1024
DFF = 2816
HM = 8
IN_COLS = 6304
EPS = 1e-6
ENG = ("pe", "act", "dve", "pool", "sp")
N_DMA_SEMS = 32
N_SP_SEMS = 24
SB_BYTES = 207 * 1024


class Ev:
    __slots__ = ("kind", "key", "val", "needed")

    def __init__(self, kind, key):
        self.kind, self.key, self.val, self.needed = kind, key, None, False


class Rec:
    __slots__ = ("lo", "hi", "w", "ev", "eng")

    def __init__(self, lo, hi, w, ev, eng):
        self.lo, self.hi, self.w, self.ev, self.eng = lo, hi, w, ev, eng


class Sched:
    def __init__(self, nc):
        self.nc = nc
        self.q = {e: [] for e in ENG}
        self.recs = {}
        self.dma_rr = 0
        self.dma_rr_pool = 0
        self.dma_last = [None] * N_DMA_SEMS
        self.dma_count = [0] * N_DMA_SEMS

    def _deps(self, eng, reads, writes, ev):
        deps = []
        for (key, lo, hi) in reads:
            lst = self.recs.setdefault(key, [])
            for r in lst:
                if r.w and r.lo < hi and lo < r.hi:
                    deps.append(r.ev)
            lst[:] = [r for r in lst if not ((not r.w) and r.eng == eng and r.lo == lo and r.hi == hi
                                             and r.ev.kind == 'eng' and ev.kind == 'eng')]
            lst.append(Rec(lo, hi, False, ev, eng))
        for (key, lo, hi) in writes:
            lst = self.recs.setdefault(key, [])
            for r in lst:
                if r.lo < hi and lo < r.hi:
                    deps.append(r.ev)
            lst[:] = [r for r in lst if not (lo <= r.lo and r.hi <= hi)]
            lst.append(Rec(lo, hi, True, ev, eng))
        out, seen = [], set()
        for d in deps:
            if d is ev or id(d) in seen:
                continue
            seen.add(id(d))
            out.append(d)
        return out

    def op(self, eng, fn, reads=(), writes=()):
        ev = Ev('eng', eng)
        deps = self._deps(eng, reads, writes, ev)
        if eng == 'pe':
            deps = [d for d in deps if not (d.kind == 'eng' and d.key == 'pe')]
        for d in deps:
            d.needed = True
        self.q[eng].append(('op', fn, deps, ev))
        return ev

    def dma(self, queue, out, in_, reads=(), writes=(), **kw):
        if queue == 'pool':
            k = N_SP_SEMS + self.dma_rr_pool
            self.dma_rr_pool = (self.dma_rr_pool + 1) % (N_DMA_SEMS - N_SP_SEMS)
        else:
            k = self.dma_rr
            self.dma_rr = (self.dma_rr + 1) % N_SP_SEMS
        self.dma_count[k] += 1
        ev = Ev('dma', k)
        ev.val = 16 * self.dma_count[k]
        deps = self._deps(queue, reads, writes, ev)
        if self.dma_last[k] is not None:
            deps.append(self.dma_last[k])
        self.dma_last[k] = ev
        for d in deps:
            d.needed = True
        self.q[queue].append(('dma', (out, in_, kw), deps, ev))
        return ev

    def emit(self):
        nc = self.nc
        for e in ENG:
            c = 0
            for item in self.q[e]:
                ev = item[3]
                if ev.kind == 'eng' and ev.needed:
                    c += 1
                    ev.val = c
        with ExitStack() as st:
            esem = {e: st.enter_context(nc.semaphore("s_" + e)) for e in ENG}
            dsem = [st.enter_context(nc.semaphore("d_%d" % i)) for i in range(N_DMA_SEMS)]
            block = st.enter_context(nc.Block())

            def run(engname, engine):
                seen = {}
                for kind, fn, deps, ev in self.q[engname]:
                    for d in deps:
                        key = (d.kind, d.key)
                        if seen.get(key, 0) >= d.val:
                            continue
                        seen[key] = d.val
                        engine.wait_ge(esem[d.key] if d.kind == 'eng' else dsem[d.key], d.val)
                    if kind == 'op':
                        ins = fn(engine)
                        if ev.needed:
                            ins.then_inc(esem[engname], 1)
                    else:
                        out, in_, kw = fn
                        engine.dma_start(out=out, in_=in_, **kw).then_inc(dsem[ev.key], 16)

            @block.tensor
            def _(e):
                run("pe", e)

            @block.scalar
            def _(e):
                run("act", e)

            @block.vector
            def _(e):
                run("dve", e)

            @block.gpsimd
            def _(e):
                run("pool", e)

            @block.sync
            def _(e):
                run("sp", e)
                for k in range(N_DMA_SEMS):
                    if self.dma_count[k]:
                        e.wait_ge(dsem[k], 16 * self.dma_count[k])


def _esz(dt):
    return 2 if dt == BF16 else 4


def reg(ap):
    t = ap.tensor
    row = 1
    for s in list(t.shape)[1:]:
        row *= int(s)
    es = _esz(t.dtype)
    f0 = (int(ap.offset) % row) * es
    ext = 1
    for step, cnt in list(ap.ap)[1:]:
        ext += (int(cnt) - 1) * abs(int(step))
    lo, hi = f0, f0 + ext * es
    if t.name == "psum":
        lo = lo // 2048 * 2048
        hi = (hi + 2047) // 2048 * 2048
    return (t.name, lo, hi)


class Bld:
    def __init__(self, T, n_layers=2, debug=False):
        self.T = T
        self.NG = T // 512
        self.L = n_layers
        self.debug = debug
        nc = self.nc = bass.Bass("TRN2", target_bir_lowering=False)
        self.S = Sched(nc)
        self.inp = {}
        self.scr = {}

    def din(self, name, shape, dt=F32):
        self.inp[name] = self.nc.dram_tensor(name, list(shape), dt, kind="ExternalInput").ap()
        return self.inp[name]

    def dscr(self, name, shape, dt=F32):
        kind = "ExternalOutput" if self.debug else "Internal"
        self.scr[name] = self.nc.dram_tensor(name, list(shape), dt, kind=kind).ap()
        return self.scr[name]

    def reset(self):
        self.ptr = self.persist

    def alloc(self, dt, *free):
        n = 1
        for f in free:
            n *= f
        nb = (n * _esz(dt) + 63) // 64 * 64
        off = self.ptr
        self.ptr += nb
        assert self.ptr <= SB_BYTES, ("SBUF overflow", self.ptr)
        ap = self.big[:, off // 4:(off + nb) // 4]
        if dt != F32:
            ap = ap.bitcast(dt)
        ap = ap[:, 0:n]
        if len(free) == 2:
            ap = ap.rearrange("p (a b) -> p a b", a=free[0])
        elif len(free) == 3:
            ap = ap.rearrange("p (a b c) -> p a b c", a=free[0], b=free[1])
        return ap

    def f32(self, *free):
        return self.alloc(F32, *free)

    def b16(self, *free):
        return self.alloc(BF16, *free)

    def bank(self, i, parts=128, n=512):
        return self.psum[0:parts, i * 512:i * 512 + n]

    def mm(self, out, lhsT, rhs, start=True, stop=True):
        self.S.op('pe', lambda e: e.matmul(out, lhsT=lhsT, rhs=rhs, start=start, stop=stop),
                  reads=[reg(lhsT), reg(rhs)], writes=[reg(out)])

    def tr(self, out, in_, ident):
        self.S.op('pe', lambda e: e.transpose(out=out, in_=in_, identity=ident),
                  reads=[reg(in_), reg(ident)], writes=[reg(out)])

    def act(self, out, in_, func, bias=0.0, scale=1.0):
        rd = [reg(in_)]
        if not isinstance(bias, float):
            rd.append(reg(bias))
        if not isinstance(scale, float):
            rd.append(reg(scale))
        self.S.op('act', lambda e: e.activation(out=out, in_=in_, func=func, bias=bias, scale=scale),
                  reads=rd, writes=[reg(out)])

    def tt(self, eng, out, in0, in1, op):
        self.S.op(eng, lambda e: e.tensor_tensor(out=out, in0=in0, in1=in1, op=op),
                  reads=[reg(in0), reg(in1)], writes=[reg(out)])

    def ts(self, eng, out, in0, s1, op0, s2=None, op1=None):
        rd = [reg(in0)]
        if not isinstance(s1, float):
            rd.append(reg(s1))
        if s2 is not None and not isinstance(s2, float):
            rd.append(reg(s2))
        if op1 is None:
            fn = lambda e: e.tensor_scalar(out=out, in0=in0, scalar1=s1, scalar2=None, op0=op0)
        else:
            fn = lambda e: e.tensor_scalar(out=out, in0=in0, scalar1=s1, scalar2=s2, op0=op0, op1=op1)
        self.S.op(eng, fn, reads=rd, writes=[reg(out)])

    def stt(self, out, in0, scalar, in1, op0, op1):
        rd = [reg(in0), reg(in1)]
        if not isinstance(scalar, float):
            rd.append(reg(scalar))
        self.S.op('dve', lambda e: e.scalar_tensor_tensor(out=out, in0=in0, scalar=scalar, in1=in1, op0=op0, op1=op1),
                  reads=rd, writes=[reg(out)])

    def copy(self, eng, out, in_):
        if eng == 'act':
            self.S.op('act', lambda e: e.copy(out=out, in_=in_), reads=[reg(in_)], writes=[reg(out)])
        else:
            self.S.op(eng, lambda e: e.tensor_copy(out=out, in_=in_), reads=[reg(in_)], writes=[reg(out)])

    def memset(self, eng, out, val):
        self.S.op(eng, lambda e: e.memset(out, val), writes=[reg(out)])

    def recip(self, out, in_):
        self.S.op('dve', lambda e: e.reciprocal(out=out, in_=in_), reads=[reg(in_)], writes=[reg(out)])

    def scan(self, out, d0, d1, init, op0, op1):
        self.S.op('dve', lambda e: e.tensor_tensor_scan(out=out, data0=d0, data1=d1, initial=init, op0=op0, op1=op1),
                  reads=[reg(d0), reg(d1)], writes=[reg(out)])

    def load(self, q, out, in_, key=None, slow=False):
        kw = {"allow_slow_non_contiguous": True} if slow else {}
        self.S.dma(q, out, in_, reads=[key] if key else [], writes=[reg(out)], **kw)

    def store(self, q, out, in_, key):
        self.S.dma(q, out, in_, reads=[reg(in_)], writes=[key])

    def rms_rstd(self, x_tiles, P, nfeat, N, pbank, sqb, sr, rstd, eps=EPS):
        ps = self.bank(pbank, P, N)
        n = len(x_tiles)
        for i, xt in enumerate(x_tiles):
            sq = sqb[i % 2]
            self.act(sq, xt, AF.Square)
            self.mm(ps, self.ones[0:P, 0:P], sq, start=(i == 0), stop=(i == n - 1))
        self.act(sr, ps, AF.Sqrt, bias=self.eps_ap(P, eps), scale=1.0 / nfeat)
        self.recip(rstd, sr)

    def eps_ap(self, P, eps):
        if eps == EPS:
            return self.epsc[0:P, 0:1]
        return self.epsc[0:P, 1:2]


def build(T, n_layers=2, debug=False, phases=None):
    B = Bld(T, n_layers, debug)
    nc, S = B.nc, B.S
    NG = B.NG
    L = 2
    din = B.din
    x_in = din("x", [T, D])
    pos_in = din("positions", [1, T], I32)
    W = {}
    shapes = {
        "ffn1_norm": [L, D], "ffn1_w_gate": [L, D, DFF], "ffn1_w_up": [L, D, DFF], "ffn1_w_down": [L, DFF, D],
        "mix_norm": [L, D], "w_in": [L, D, IN_COLS], "mla_q_a_norm": [L, 256], "mla_w_uq": [L, 256, 768],
        "mla_kv_a_norm": [L, 128], "mla_w_ukv": [L, 128, 1024], "mla_q_norm": [L, 96], "mla_k_norm": [L, 96],
        "sg_v_norm": [L, 512], "sg_w_s": [L, 8, 128, 128], "sg_b_s": [L, 8, 128],
        "rw_mu": [L, 1792], "rw_w0": [L, 512], "rw_w2": [L, 64, 512], "rw_a0": [L, 512], "rw_a2": [L, 64, 512],
        "rw_g2": [L, 128, 512], "rw_k_k": [L, 512], "rw_k_a": [L, 512], "rw_r_k": [L, 8, 64],
        "rw_ln_g": [L, 512], "rw_ln_b": [L, 512], "rw_v0": [1, 512], "rw_v1": [1, 512, 32], "rw_v2": [1, 32, 512],
        "w_out_mla": [L, 512, D], "w_out_sg": [L, 512, D], "w_out_rw": [L, 512, D], "w_o": [L, D, D],
        "ffn2_norm": [L, D], "ffn2_w_gate": [L, D, DFF], "ffn2_w_up": [L, D, DFF], "ffn2_w_down": [L, DFF, D],
    }
    for k, shp in shapes.items():
        W[k] = din(k, shp)
    c_ident = din("c_ident", [128, 128])
    c_m64 = din("c_m64", [64, 4, 64])
    c_amask = din("c_amask", [128, 4, 512])
    c_tril = din("c_tril", [128, 128])
    c_rope = din("c_rope", [96, 4])
    c_esel = din("c_esel", [32, 2, 96])
    c_reset = din("c_reset", [64, 256])
    out_d = nc.dram_tensor("out", [T, D], F32, kind="ExternalOutput").ap()

    hT = B.dscr("hT", [D, T])
    cqn_d = B.dscr("cqn", [256, T], BF16)
    ckvn_d = B.dscr("ckvn", [128, T], BF16)
    kpe_d = B.dscr("kpe", [32, T], BF16)
    prw_d = B.dscr("prw", [1792, T])
    gts_d = B.dscr("gts", [3072, T])
    yaT_d = B.dscr("yaT", [512, T], BF16)
    ybT_d = B.dscr("ybT", [512, T], BF16)
    ycT_d = B.dscr("ycT", [512, T], BF16)
    vf_d = B.dscr("vfirst", [512, T])

    st = ExitStack()
    with st:
        bigt = st.enter_context(nc.sbuf_tensor("big", [128, SB_BYTES // 4], F32))
        B.big = bigt
        pst = st.enter_context(nc.psum_tensor("psum", [128, 4096], F32))
        B.psum = pst
        B.ptr = 0
        B.ident = B.f32(128)
        B.ones = B.f32(128)
        B.epsc = B.f32(4)
        B.load('sp', B.ident, c_ident)
        B.memset('pool', B.ones, 1.0)
        B.memset('pool', B.epsc[:, 0:1], EPS)
        B.memset('pool', B.epsc[:, 1:2], 64e-5)
        B.memset('pool', B.epsc[:, 2:3], 0.0)
        B.memset('pool', B.epsc[:, 3:4], 1e-30)
        B.persist = B.ptr

        def hview(g):
            return hT.rearrange("(dt p) t -> p dt t", p=128)[:, :, g * 512:(g + 1) * 512]

        def phase_x():
            B.reset()
            xt = [B.f32(D), B.f32(D)]
            ho = [B.f32(8, 512), B.f32(8, 512)]
            for g in range(NG):
                hg = ho[g % 2]
                for tt_ in range(4):
                    ti = g * 4 + tt_
                    xb = xt[ti % 2]
                    B.load('sp', xb, x_in[ti * 128:(ti + 1) * 128, :])
                    for half in range(2):
                        pb = B.psum[:, (ti % 2) * 1024 + half * 512:(ti % 2) * 1024 + half * 512 + 512]
                        for j in range(4):
                            dt_ = half * 4 + j
                            B.tr(pb[:, j * 128:(j + 1) * 128], xb[:, dt_ * 128:(dt_ + 1) * 128], B.ident)
                        B.copy('dve' if half == 0 else 'act', hg[:, half * 4:(half + 1) * 4, tt_ * 128:(tt_ + 1) * 128],
                               pb.rearrange("p (a b) -> p a b", a=4))
                B.store('sp', hview(g), hg, ("hT", g * 512, (g + 1) * 512))

        def phase_o():
            B.reset()
            hi = [B.f32(8, 512), B.f32(8, 512)]
            xo = [B.f32(D), B.f32(D)]
            for g in range(NG):
                hg = hi[g % 2]
                B.load('sp', hg, hview(g), ("hT", g * 512, (g + 1) * 512))
                for tt_ in range(4):
                    ti = g * 4 + tt_
                    xb = xo[ti % 2]
                    for half in range(2):
                        pb = B.psum[:, (ti % 2) * 1024 + half * 512:(ti % 2) * 1024 + half * 512 + 512]
                        for j in range(4):
                            dt_ = half * 4 + j
                            B.tr(pb[:, j * 128:(j + 1) * 128], hg[:, dt_, tt_ * 128:(tt_ + 1) * 128], B.ident)
                        B.copy('dve' if half == 0 else 'act', xb[:, half * 512:(half + 1) * 512], pb)
                    B.store('sp', out_d[ti * 128:(ti + 1) * 128, :], xb, ("out", ti, ti + 1))

        def phase_ffn(l, nm):
            B.reset()
            Wg = B.b16(8, DFF)
            Wu = B.b16(8, DFF)
            Wd = B.b16(22, D)
            wgv = W[nm + "_w_gate"][l].rearrange("(kt p) f -> p kt f", p=128)
            wuv = W[nm + "_w_up"][l].rearrange("(kt p) f -> p kt f", p=128)
            wdv = W[nm + "_w_down"][l].rearrange("(kt p) f -> p kt f", p=128)
            for c0, c1 in ((0, 256), (256, 896), (896, 1792), (1792, 2816)):
                B.load('pool', Wg[:, :, c0:c1], wgv[:, :, c0:c1])
                B.load('pool', Wu[:, :, c0:c1], wuv[:, :, c0:c1])
            for k0, k1 in ((0, 6), (6, 14), (14, 22)):
                B.load('pool', Wd[:, k0:k1, :], wdv[:, k0:k1, :])
            gain = B.f32(8)
            B.load('sp', gain, W[nm + "_norm"][l].rearrange("(dt p) -> p dt", p=128), slow=True)
            hb = [B.f32(8, 512), B.f32(8, 512)]
            sqb = [B.f32(512), B.f32(512)]
            sr = sqb[0]
            rstd = B.f32(512)
            xn = B.b16(8, 512)
            actT = B.b16(22, 512)
            sgb = sqb
            for g in range(NG):
                h = hb[g % 2]
                B.load('sp', h, hview(g), ("hT", g * 512, (g + 1) * 512))
                B.rms_rstd([h[:, kt, :] for kt in range(8)], 128, D, 512, 7, sqb, sr, rstd)
                for kt in range(8):
                    B.stt(xn[:, kt, :], h[:, kt, :], gain[:, kt:kt + 1], rstd, ALU.mult, ALU.mult)
                for ft in range(22):
                    pg = B.bank((2 * ft) % 6)
                    pu = B.bank((2 * ft + 1) % 6)
                    for kt in range(8):
                        B.mm(pg, Wg[:, kt, ft * 128:(ft + 1) * 128], xn[:, kt, :], kt == 0, kt == 7)
                    for kt in range(8):
                        B.mm(pu, Wu[:, kt, ft * 128:(ft + 1) * 128], xn[:, kt, :], kt == 0, kt == 7)
                    sg = sgb[ft % 2]
                    B.act(sg, pg, AF.Silu)
                    B.tt('dve', actT[:, ft, :], sg, pu, ALU.mult)
                for dt_ in range(8):
                    pd = B.bank(dt_ % 6)
                    for ft in range(22):
                        B.mm(pd, Wd[:, ft, dt_ * 128:(dt_ + 1) * 128], actT[:, ft, :], ft == 0, ft == 21)
                    B.stt(h[:, dt_, :], pd, 0.5, h[:, dt_, :], ALU.mult, ALU.add)
                B.store('sp', hview(g), h, ("hT", g * 512, (g + 1) * 512))

        B.phase_x, B.phase_o, B.phase_ffn = phase_x, phase_o, phase_ffn
        B.W, B.hT, B.hview = W, hT, hview
        B.c = dict(m64=c_m64, amask=c_amask, tril=c_tril, rope=c_rope, esel=c_esel, reset=c_reset)
        B.d = dict(cqn=cqn_d, ckvn=ckvn_d, kpe=kpe_d, prw=prw_d, gts=gts_d, yaT=yaT_d, ybT=ybT_d, ycT=ycT_d, vf=vf_d,
                   pos=pos_in)

        if phases is None:
            phases = ["x"] + sum([[("ffn", l, "ffn1"), ("proj", l), ("mla", l), ("rwkv", l), ("merge", l),
                                  ("ffn", l, "ffn2")] for l in range(n_layers)], []) + ["o"]
        for ph in phases:
            if ph == "x":
                phase_x()
            elif ph == "o":
                phase_o()
            elif ph[0] == "ffn":
                phase_ffn(ph[1], ph[2])
            elif ph[0] == "proj":
                phase_proj(B, ph[1])
            elif ph[0] == "mla":
                phase_mla(B, ph[1])
            elif ph[0] == "rwkv":
                phase_rwkv(B, ph[1])
            elif ph[0] == "merge":
                phase_merge(B, ph[1])
        S.emit()
    return B


def bcast_rows(ap1d, nparts, n):
    return bass.AP(tensor=ap1d.tensor, offset=ap1d.offset, ap=[[0, nparts], [1, n]])


def phase_proj(B, l):
    T, NG, W = B.T, B.NG, B.W
    B.reset()
    Win = B.b16(8, IN_COLS)
    wv = W["w_in"][l].rearrange("(kt p) f -> p kt f", p=128)
    for c0, c1 in ((0, 416), (1440, 2336), (2336, 3232), (3232, 4768), (4768, 6304), (416, 1440)):
        B.load('pool', Win[:, :, c0:c1], wv[:, :, c0:c1])
    gain = B.f32(8)
    B.load('sp', gain, W["mix_norm"][l].rearrange("(dt p) -> p dt", p=128), slow=True)
    qag = B.f32(2)
    B.load('sp', qag, W["mla_q_a_norm"][l].rearrange("(dt p) -> p dt", p=128), slow=True)
    kvg = B.f32(1)
    B.load('sp', kvg, W["mla_kv_a_norm"][l].rearrange("(dt p) -> p dt", p=128), slow=True)
    wsn = B.f32(8, 128)
    B.load('sp', wsn, W["sg_w_s"][l].rearrange("g t s -> t g s"))
    tril = B.f32(128)
    B.load('sp', tril, B.c["tril"])
    B.tt('pool', wsn, wsn, tril.unsqueeze(1).to_broadcast([128, 8, 128]), ALU.mult)
    WsT = B.b16(8, 128)
    for half in range(2):
        pb = B.bank(4 + half)
        for j in range(4):
            B.tr(pb[:, j * 128:(j + 1) * 128], wsn[:, half * 4 + j, :], B.ident)
        B.copy('dve', WsT[:, half * 4:(half + 1) * 4, :], pb.rearrange("p (a b) -> p a b", a=4))
    sgb_ = B.f32(8)
    B.load('sp', sgb_, W["sg_b_s"][l].rearrange("g t -> t g"), slow=True)
    vgain = B.f32(512)
    B.load('sp', vgain, bcast_rows(W["sg_v_norm"][l], 128, 512))
    h = B.f32(8, 512)
    sqb = [B.f32(512), B.f32(512)]
    sr = B.f32(512)
    rstd = B.f32(512)
    z = B.b16(8, 512)
    gst = [B.f32(4, 512), B.f32(4, 512)]
    rst = [B.f32(4, 512), B.f32(4, 512)]
    cqf = B.f32(2, 512)
    cqb = B.b16(2, 512)
    ckb = B.b16(512)
    kpb = B.b16(512)
    xs = B.f32(512)
    x2 = B.f32(512)
    gu = B.f32(512)
    gv = B.f32(512)
    junk = B.f32(512)
    ss = B.f32(2)
    vn = B.b16(512)
    yb = B.f32(512)
    ybs = B.b16(4, 512)
    prw_v = B.d["prw"].rearrange("(b p) t -> p b t", p=64)
    gts_v = B.d["gts"].rearrange("(j p) t -> p j t", p=128)
    rot = [0]

    def nb():
        rot[0] = (rot[0] + 1) % 4
        return rot[0]

    def gelu(dst, src_ps):
        B.copy('act', xs, src_ps)
        B.act(x2, src_ps, AF.Square)
        B.ts('dve', x2, x2, 0.044715, ALU.mult, 1.0, ALU.add)
        B.tt('dve', x2, x2, xs, ALU.mult)
        B.act(x2, x2, AF.Sigmoid, scale=1.5957691216057308)
        B.tt('dve', dst, xs, x2, ALU.mult)

    for g in range(NG):
        gc = slice(g * 512, (g + 1) * 512)
        B.load('sp', h, B.hview(g), ("hT", g * 512, (g + 1) * 512))
        B.rms_rstd([h[:, kt, :] for kt in range(8)], 128, D, 512, 7, sqb, sr, rstd)
        for kt in range(8):
            B.stt(z[:, kt, :], h[:, kt, :], gain[:, kt:kt + 1], rstd, ALU.mult, ALU.mult)

        def proj(c0, M):
            pb = B.bank(nb(), M)
            for kt in range(8):
                B.mm(pb, Win[:, kt, c0:c0 + M], z[:, kt, :], kt == 0, kt == 7)
            return pb
        for i in range(2):
            pb = proj(i * 128, 128)
            B.copy('act', cqf[:, i, :], pb)
        B.rms_rstd([cqf[:, 0, :], cqf[:, 1, :]], 128, 256, 512, 7, sqb, sr, rstd)
        for i in range(2):
            B.stt(cqb[:, i, :], cqf[:, i, :], qag[:, i:i + 1], rstd, ALU.mult, ALU.mult)
        B.store('sp', B.d["cqn"].rearrange("(i p) t -> p i t", p=128)[:, :, gc], cqb, ("cqn", g * 512, (g + 1) * 512))
        pb = proj(256, 128)
        B.copy('act', cqf[:, 0, :], pb)
        B.rms_rstd([cqf[:, 0, :]], 128, 128, 512, 7, sqb, sr, rstd)
        B.stt(ckb, cqf[:, 0, :], kvg[:, 0:1], rstd, ALU.mult, ALU.mult)
        B.store('sp', B.d["ckvn"][:, gc], ckb, ("ckvn", g * 512, (g + 1) * 512))
        pb = proj(384, 32)
        B.copy('act', kpb[0:32, :], pb)
        B.store('sp', B.d["kpe"][:, gc], kpb[0:32, :], ("kpe", g * 512, (g + 1) * 512))
        for b in range(28):
            pb = proj(1440 + 64 * b, 64)
            stg = rst[(b // 4) % 2]
            B.copy('act' if b % 2 else 'dve', stg[0:64, b % 4, :], pb)
            if b % 4 == 3:
                B.store('sp', prw_v[:, b - 3:b + 1, gc], stg[0:64, :, :], ("prw", g * 512, (g + 1) * 512))
        for j in range(24):
            pb = proj(3232 + 128 * j, 128)
            stg = gst[(j // 4) % 2]
            B.act(stg[:, j % 4, :], pb, AF.Sigmoid)
            if j % 4 == 3:
                B.store('sp', gts_v[:, j - 3:j + 1, gc], stg, ("gts", g * 512, (g + 1) * 512))
        for tt_ in range(4):
            tc_ = slice(tt_ * 128, (tt_ + 1) * 128)
            pu = B.bank(4)
            pv = B.bank(5)
            for kt in range(8):
                B.mm(pu, z[:, kt, tc_], Win[:, kt, 416:928], kt == 0, kt == 7)
            for kt in range(8):
                B.mm(pv, z[:, kt, tc_], Win[:, kt, 928:1440], kt == 0, kt == 7)
            gelu(gu, pu)
            gelu(gv, pv)
            S_ = B.S
            S_.op('act', lambda e: e.activation(out=junk, in_=gv, func=AF.Square, accum_out=ss[:, 0:1]),
                  reads=[reg(gv)], writes=[reg(junk), reg(ss[:, 0:1])])
            B.act(ss[:, 1:2], ss[:, 0:1], AF.Sqrt, bias=B.epsc[:, 0:1], scale=1.0 / 512)
            B.recip(ss[:, 1:2], ss[:, 1:2])
            B.stt(vn, gv, ss[:, 1:2], vgain, ALU.mult, ALU.mult)
            pm = B.bank(6)
            for gi in range(8):
                B.mm(pm[:, gi * 64:(gi + 1) * 64], WsT[:, gi, :], vn[:, gi * 64:(gi + 1) * 64])
            for gi in range(8):
                B.stt(yb[:, gi * 64:(gi + 1) * 64], pm[:, gi * 64:(gi + 1) * 64], sgb_[:, gi:gi + 1],
                      gu[:, gi * 64:(gi + 1) * 64], ALU.add, ALU.mult)
            pt = B.bank(7)
            for j in range(4):
                B.tr(pt[:, j * 128:(j + 1) * 128], yb[:, j * 128:(j + 1) * 128], B.ident)
            B.copy('act', ybs[:, :, tc_], pt.rearrange("p (a b) -> p a b", a=4))
        B.store('sp', B.d["ybT"].rearrange("(j p) t -> p j t", p=128)[:, :, gc], ybs, ("ybT", g * 512, (g + 1) * 512))


def phase_merge(B, l):
    T, NG, W = B.T, B.NG, B.W
    B.reset()
    Wa = B.b16(4, D)
    Wb = B.b16(4, D)
    Wc = B.b16(8, D)
    Wo = B.b16(8, D)
    B.load('pool', Wa, W["w_out_mla"][l].rearrange("(kt p) f -> p kt f", p=128))
    B.load('pool', Wb, W["w_out_sg"][l].rearrange("(kt p) f -> p kt f", p=128))
    B.load('pool', Wc[0:64], W["w_out_rw"][l].rearrange("(h p) f -> p h f", p=64))
    B.load('pool', Wo, W["w_o"][l].rearrange("(kt p) f -> p kt f", p=128))
    h = B.f32(8, 512)
    ya = B.b16(4, 512)
    yb = B.b16(4, 512)
    yc = B.b16(8, 512)
    gl = [B.f32(3, 512), B.f32(3, 512)]
    m = B.b16(8, 512)
    t0 = B.f32(512)
    t1 = B.f32(512)
    t2 = B.f32(512)
    gts_v = B.d["gts"].rearrange("(br dt p) t -> p br dt t", br=3, p=128)
    for g in range(NG):
        gc = slice(g * 512, (g + 1) * 512)
        key = lambda n: (n, g * 512, (g + 1) * 512)
        B.load('sp', h, B.hview(g), key("hT"))
        B.load('sp', ya, B.d["yaT"].rearrange("(j p) t -> p j t", p=128)[:, :, gc], key("yaT"))
        B.load('sp', yb, B.d["ybT"].rearrange("(j p) t -> p j t", p=128)[:, :, gc], key("ybT"))
        B.load('sp', yc[0:64], B.d["ycT"].rearrange("(h p) t -> p h t", p=64)[:, :, gc], key("ycT"))
        for dt_ in range(8):
            dc = slice(dt_ * 128, (dt_ + 1) * 128)
            glt = gl[dt_ % 2]
            B.load('sp', glt, gts_v[:, :, dt_, gc], key("gts"))
            pa = B.bank((3 * dt_) % 6)
            pb = B.bank((3 * dt_ + 1) % 6)
            pc = B.bank((3 * dt_ + 2) % 6)
            for kt in range(4):
                B.mm(pa, Wa[:, kt, dc], ya[:, kt, :], kt == 0, kt == 3)
            for kt in range(4):
                B.mm(pb, Wb[:, kt, dc], yb[:, kt, :], kt == 0, kt == 3)
            for hh in range(8):
                B.mm(pc, Wc[0:64, hh, dc], yc[0:64, hh, :], hh == 0, hh == 7)
            B.tt('dve', t0, pa, glt[:, 0, :], ALU.mult)
            B.tt('dve', t1, pb, glt[:, 1, :], ALU.mult)
            B.tt('dve', t2, pc, glt[:, 2, :], ALU.mult)
            B.tt('pool', t0, t0, t1, ALU.add)
            B.tt('pool', m[:, dt_, :], t0, t2, ALU.add)
        for dt_ in range(8):
            dc = slice(dt_ * 128, (dt_ + 1) * 128)
            po = B.bank(6 + dt_ % 2)
            for kt in range(8):
                B.mm(po, Wo[:, kt, dc], m[:, kt, :], kt == 0, kt == 7)
            B.tt('dve', h[:, dt_, :], h[:, dt_, :], po, ALU.add)
        B.store('sp', B.hview(g), h, key("hT"))


def phase_mla(B, l):
    T, NG, W = B.T, B.NG, B.W
    NT = T // 128
    B.reset()
    cqn = B.b16(2, T)
    ckvn = B.b16(T)
    kpe = B.b16(T)
    B.load('sp', cqn, B.d["cqn"].rearrange("(i p) t -> p i t", p=128), ("cqn", 0, T))
    B.load('sp', ckvn, B.d["ckvn"], ("ckvn", 0, T))
    B.load('sp', kpe[0:32, :], B.d["kpe"], ("kpe", 0, T))
    Ct = B.f32(T)
    Sn = B.f32(T)
    Wq = B.b16(2, 8, 96)
    Wqs = B.b16(2, 8, 96)
    Wk = B.b16(8, 96)
    Wv = B.b16(8, 64)
    esel = B.b16(2, 96)
    wq_v = W["mla_w_uq"][l].rearrange("(kt p) (h d) -> p kt h d", p=128, h=8)
    for kt in range(2):
        B.load('pool', Wq[:, kt], wq_v[:, kt])
        B.load('pool', Wqs[:, kt, :, 0:64], wq_v[:, kt, :, 0:64])
        B.load('pool', Wqs[:, kt, :, 64:80], wq_v[:, kt, :, 80:96])
        B.load('pool', Wqs[:, kt, :, 80:96], wq_v[:, kt, :, 64:80])
    wkv_v = W["mla_w_ukv"][l].rearrange("p (h d) -> p h d", h=8)
    B.memset('pool', Wk, 0.0)
    B.load('pool', Wk[:, :, 0:64], wkv_v[:, :, 0:64])
    B.load('pool', Wv, wkv_v[:, :, 64:128])
    B.load('pool', esel[0:32], B.c["esel"])
    gq = B.f32(2)
    gk = B.f32(2)
    for gt_, nm in ((gq, "mla_q_norm"), (gk, "mla_k_norm")):
        v = W[nm][l].rearrange("(d o) -> d o", o=1)
        B.load('sp', gt_[0:96, 0:1], v)
        B.load('sp', gt_[0:64, 1:2], v[0:64])
        B.load('sp', gt_[64:80, 1:2], v[80:96])
        B.load('sp', gt_[80:96, 1:2], v[64:80])
    ropec = B.f32(4)
    B.load('sp', ropec[0:96], B.c["rope"])
    amask = B.b16(4, 512)
    B.load('pool', amask, B.c["amask"])
    p0 = B.ptr
    posi = B.alloc(I32, T)
    ang = B.f32(T)
    u = B.f32(T)
    ki = B.alloc(I32, T)
    B.ptr = p0
    Vsb = B.b16(NT, 8, 65)
    qT = B.b16(T)
    kT = B.b16(T)
    qa = B.f32(512)
    sq = B.f32(512)
    sr = B.f32(512)
    rstd = B.f32(512)
    t1 = B.f32(512)
    t2 = B.f32(512)
    PT = [B.b16(512) for _ in range(3)]
    oT = B.f32(512)
    rec = B.f32(512)
    yas = [B.b16(512), B.b16(512)]
    P96 = slice(0, 96)
    B.load('sp', posi[P96], bcast_rows(B.d["pos"][0], 96, T))
    B.copy('dve', u[P96], posi[P96])
    B.ts('dve', ang[P96], u[P96], ropec[P96, 0:1], ALU.mult)
    TWO_PI = 2.0 * math.pi
    C1 = 6.28125
    C2 = TWO_PI - C1
    for tab, shift in ((Ct, math.pi / 2), (Sn, 0.0)):
        B.ts('dve', u[P96], ang[P96], float(shift), ALU.add, float(1.0 / TWO_PI), ALU.mult)
        B.copy('dve', ki[P96], u[P96])
        B.copy('dve', u[P96], ki[P96])
        B.stt(tab[P96], u[P96], -C1, ang[P96], ALU.mult, ALU.add)
        B.stt(tab[P96], u[P96], -C2, tab[P96], ALU.mult, ALU.add)
        if shift:
            B.ts('dve', tab[P96], tab[P96], float(shift), ALU.add)
        B.ts('dve', tab[P96], tab[P96], 3.1415925, ALU.min, -3.1415925, ALU.max)
        B.act(tab[P96], tab[P96], AF.Sin)
    B.ts('dve', Sn[P96], Sn[P96], ropec[P96, 1:2], ALU.mult)
    B.memset('pool', Vsb[:, :, :, 64:65], 1.0)
    for tt_ in range(NT):
        pv = B.bank(4 + tt_ % 2)
        B.mm(pv, ckvn[:, tt_ * 128:(tt_ + 1) * 128], Wv.rearrange("p h d -> p (h d)"))
        B.copy('act' if tt_ % 2 else 'dve', Vsb[:, tt_, :, 0:64], pv.rearrange("p (h d) -> p h d", h=8))
    scale = 96.0 ** -0.5

    tmp2 = [B.f32(512) for _ in range(6)]
    tmps = [(qa, sq, sr, rstd, t1, t2), tuple(tmp2)]

    def qk_stages(h, g, is_q):
        gc = slice(g * 512, (g + 1) * 512)
        qa_, sq_, sr_, rstd_, t1_, t2_ = tmps[0 if is_q else 1]
        if is_q:
            pa, pb, p6 = B.bank(4, 96), B.bank(5, 96), B.bank(6, 96)
            gn, dst = gq, qT
        else:
            pa, pb, p6 = B.bank(2, 96), B.bank(3, 96), B.bank(7, 96)
            gn, dst = gk, kT

        def s0():
            if is_q:
                for pp, ww in ((pa, Wq), (pb, Wqs)):
                    for kt in range(2):
                        B.mm(pp, ww[:, kt, h, :], cqn[:, kt, gc], kt == 0, kt == 1)
            else:
                for pp, si in ((pa, 0), (pb, 1)):
                    B.mm(pp, Wk[:, h, :], ckvn[:, gc], True, False)
                    B.mm(pp, esel[0:32, si, :], kpe[0:32, gc], False, True)

        def s1():
            B.act(sq_[P96], pa, AF.Square)
            B.copy('act', qa_[P96], pa)

        def s2():
            B.mm(p6, B.ones[0:96, 0:96], sq_[P96])
            B.stt(t1_[P96], qa_[P96], gn[P96, 0:1], Ct[P96, gc], ALU.mult, ALU.mult)
            B.stt(t2_[P96], pb, gn[P96, 1:2], Sn[P96, gc], ALU.mult, ALU.mult)

        def s3():
            B.act(sr_[P96], p6, AF.Sqrt, bias=B.epsc[0:96, 0:1], scale=1.0 / 96)
            B.tt('dve', t1_[P96], t1_[P96], t2_[P96], ALU.add)

        def s4():
            B.recip(rstd_[P96], sr_[P96])
            B.tt('dve', dst[P96, gc], t1_[P96], rstd_[P96], ALU.mult)
        return [s0, s1, s2, s3, s4]

    for h in range(8):
        for g in range(NG):
            for fq, fk in zip(qk_stages(h, g, True), qk_stages(h, g, False)):
                fq()
                fk()
        for g in range(NG):
            gc = slice(g * 512, (g + 1) * 512)
            po = B.bank(2 + g % 2, 65)
            nk = 4 * (g + 1)

            def s_mm(kt):
                B.mm(B.bank(kt % 2), kT[P96, kt * 128:(kt + 1) * 128], qT[P96, gc])
            s_mm(0)
            for kt in range(nk):
                if kt + 1 < nk:
                    s_mm(kt + 1)
                pt = PT[kt % 3]
                B.act(pt, B.bank(kt % 2), AF.Exp, scale=float(scale))
                if kt >= 4 * g:
                    B.tt('dve', pt, pt, amask[:, kt - 4 * g, :], ALU.mult)
                B.mm(po, Vsb[:, kt, h, :], pt, kt == 0, kt == nk - 1)
            B.copy('act', oT[0:65], po)
            B.recip(rec[64:65], oT[64:65])
            p7 = B.bank(7, 64)
            B.mm(p7, B.ones[64:65, 0:64], rec[64:65])
            ys = yas[g % 2]
            B.tt('dve', ys[0:64], oT[0:64], p7, ALU.mult)
            B.store('sp', B.d["yaT"][h * 64:(h + 1) * 64, gc], ys[0:64], ("yaT", g * 512, (g + 1) * 512))


def phase_rwkv(B, l):
    T, W = B.T, B.W
    GR = 128
    NGR = T // GR
    E05 = math.exp(-0.5)
    P = slice(0, 64)
    B.reset()

    def v64(name, pat, **kw):
        t = B.f32(*kw.pop("shape"))
        B.load('sp', t[P], W[name][kw.pop("li", l)].rearrange(pat, **kw), slow=True)
        return t
    mu = v64("rw_mu", "(b p) -> p b", shape=(28,), p=64)
    w0 = v64("rw_w0", "(h p) -> p h", shape=(8,), p=64)
    a0 = v64("rw_a0", "(h p) -> p h", shape=(8,), p=64)
    kkc = v64("rw_k_k", "(h p) -> p h", shape=(8,), p=64)
    kac = v64("rw_k_a", "(h p) -> p h", shape=(8,), p=64)
    lng = v64("rw_ln_g", "(h p) -> p h", shape=(8,), p=64)
    lnb = v64("rw_ln_b", "(h p) -> p h", shape=(8,), p=64)
    rkc = v64("rw_r_k", "h k -> k h", shape=(8,))
    w2 = B.b16(512)
    a2 = B.b16(512)
    g2 = B.b16(2, 512)
    B.load('pool', w2[P], W["rw_w2"][l])
    B.load('pool', a2[P], W["rw_a2"][l])
    B.load('pool', g2[P], W["rw_g2"][l].rearrange("(two p) f -> p two f", p=64))
    if l > 0:
        v0 = v64("rw_v0", "(h p) -> p h", shape=(8,), p=64, li=0)
        v1 = B.f32(8, 32)
        v2 = B.f32(512)
        B.load('sp', v1[P], W["rw_v1"][0].rearrange("(h p) r -> p h r", p=64))
        B.load('sp', v2[0:32], W["rw_v2"][0])
    m64 = B.f32(4, 64)
    B.load('sp', m64[P], B.c["m64"])
    rmask = B.f32(GR)
    B.load("sp", rmask[P], B.c["reset"][:, 0:GR])
    onesm = B.f32(64)
    B.memset('pool', onesm[P], 1.0 / 64)
    identb = B.b16(64)
    B.copy('dve', identb[P], B.ident[P, 0:64])
    ST = B.f32(8, 64)
    STb = B.b16(8, 64)
    B.memset('pool', ST[P], 0.0)
    B.memset('pool', STb[P], 0.0)
    pr = B.f32(28, GR + 1)
    sig = B.f32(8, GR)
    aa = B.f32(8, GR)
    kkn = B.f32(8, GR)
    cs = B.f32(8, GR)
    X1 = B.f32(8, GR)
    X2 = B.f32(8, GR)
    tmpb = [B.f32(GR), B.f32(GR)]
    sx = B.b16(2, GR)
    txw = B.b16(GR)
    xab = B.b16(GR)
    vr = B.f32(GR)
    T1 = B.f32(8, GR)
    T2 = B.f32(8, GR)
    T3 = B.f32(8, GR)
    DB = []
    for _ in range(2):
        DB.append(dict(At=B.b16(8, GR), Bt=B.b16(8, GR), Kt=B.b16(8, GR), Rt=B.b16(8, GR), gam=B.f32(8, GR),
                       psh=B.f32(28, GR), gt=B.f32(8, GR), yT=B.f32(8, GR), yo=B.b16(8, GR)))
    bnames = ["NbaT", "Nba", "Nka", "Mbr", "Mkr", "Btok", "Ktok", "Vtok", "Pa", "PTa", "Ta", "TTa", "Pb", "PTb", "Tb", "TTb",
              "X0", "U", "X0v", "Yv"]
    fnames = ["KV", "KVS", "Sf"]
    MM = []
    for _ in range(2):
        d = {n: B.b16(8, 64) for n in bnames}
        d.update({n: B.f32(8, 64) for n in fnames})
        MM.append(d)
    rot = [0]
    evr = [0]

    def nb():
        rot[0] = (rot[0] + 1) % 8
        return B.bank(rot[0], 64)

    def ev_eng():
        evr[0] ^= 1
        return 'act' if evr[0] else 'dve'

    def bv(b):
        return b.rearrange("p (h c) -> p h c", h=8)

    def mbc(i):
        return m64[P, i:i + 1, :].to_broadcast([64, 8, 64])

    prw_v = B.d["prw"].rearrange("(b p) t -> p b t", p=64)
    vf_v = B.d["vf"].rearrange("(h p) t -> p h t", p=64)
    yc_v = B.d["ycT"].rearrange("(h p) t -> p h t", p=64)

    def bc8(t):
        return t[P, 0:8].unsqueeze(2).to_broadcast([64, 8, GR])

    def mm8(lf, rf, evac):
        pb = nb()
        for h in range(8):
            B.mm(pb[:, h * 64:(h + 1) * 64], lf(h), rf(h))
        evac(bv(pb))

    def cp(dst):
        return lambda pv: B.copy(ev_eng(), dst, pv)

    def half(ap2, i):
        return ap2

    def prep_steps(g):
        G = DB[g % 2]
        psh, At, Bt, Kt, Rt, gam, gt = G["psh"], G["At"], G["Bt"], G["Kt"], G["Rt"], G["gam"], G["gt"]
        g0 = g * GR
        key = lambda n: (n, g0, g0 + GR)
        r_ = psh[P, 0:8, :]
        k_ = psh[P, 8:16, :]
        v_ = psh[P, 16:24, :]
        st = []

        def ld():
            if g == 0:
                B.memset('pool', pr[P, :, 0:1], 0.0)
                B.load('sp', pr[P, :, 1:GR + 1], prw_v[:, :, 0:GR], ("prw", 0, GR))
            else:
                B.load('sp', pr[P, :, :], prw_v[:, :, g0 - 1:g0 + GR], ("prw", g0 - 1, g0 + GR))
            if l > 0:
                B.load('sp', T3[P], vf_v[:, :, g0:g0 + GR], key("vfirst"))
        st.append(ld)
        for b0 in range(0, 28, 4):
            def shift(b0=b0):
                for b in range(b0, b0 + 4):
                    tb = tmpb[b % 2]
                    B.tt('pool', tb[P], pr[P, b, 0:GR], pr[P, b, 1:GR + 1], ALU.subtract)
                    B.stt(psh[P, b, :], tb[P], mu[P, b:b + 1], pr[P, b, 1:GR + 1], ALU.mult, ALU.add)
            st.append(shift)

        def lora0():
            B.act(txw[P], psh[P, 24, :], AF.Tanh)
            B.act(sx[P], psh[P, 26:28, :], AF.Sigmoid)
            B.copy('dve', xab[P], psh[P, 25, :])
        st.append(lora0)
        for h0 in range(0, 8, 2):
            def lora(h0=h0):
                for h in (h0, h0 + 1):
                    hc = slice(h * 64, (h + 1) * 64)
                    pb = nb()
                    B.mm(pb[:, 0:GR], w2[P, hc], txw[P])
                    B.act(sig[P, h, :], pb[:, 0:GR], AF.Sigmoid, bias=w0[P, h:h + 1])
                    pb = nb()
                    B.mm(pb[:, 0:GR], a2[P, hc], xab[P])
                    B.act(aa[P, h, :], pb[:, 0:GR], AF.Sigmoid, bias=a0[P, h:h + 1])
                    pb = nb()
                    B.mm(pb[:, 0:GR], g2[P, 0, hc], sx[P, 0, :], True, False)
                    B.mm(pb[:, 0:GR], g2[P, 1, hc], sx[P, 1, :], False, True)
                    B.copy('dve', gt[P, h, :], pb[:, 0:GR])
            st.append(lora)
        if l == 0:
            st.append(lambda: B.store('sp', vf_v[:, :, g0:g0 + GR], v_, key("vfirst")))
        else:
            def vres0():
                pb = nb()
                for h in range(8):
                    B.mm(pb[0:32, 0:GR], v1[P, h, :], v_[:, h, :], h == 0, h == 7)
                B.copy('act', vr[0:32], pb[0:32, 0:GR])
            st.append(vres0)

            def vres1():
                for hq in range(2):
                    pb = nb()
                    for j in range(4):
                        h = hq * 4 + j
                        B.mm(pb[:, j * GR:(j + 1) * GR], v2[0:32, h * 64:(h + 1) * 64], vr[0:32])
                    for j in range(4):
                        h = hq * 4 + j
                        B.act(X2[P, h, :], pb[:, j * GR:(j + 1) * GR], AF.Sigmoid, bias=v0[P, h:h + 1])
            st.append(vres1)

            def vres2():
                B.tt('pool', T3[P], T3[P], v_, ALU.subtract)
                B.tt('pool', T3[P], T3[P], X2[P], ALU.mult)
                B.tt('dve', v_, v_, T3[P], ALU.add)
            st.append(vres2)

        def kk0():
            B.tt('dve', kkn[P], k_, bc8(kkc), ALU.mult)
            B.tt('pool', X1[P], kkn[P], kkn[P], ALU.mult)
        st.append(kk0)

        def kk1():
            for i in range(2):
                pb = nb()
                B.mm(pb, B.ones[P, 0:64], X1[P, 4 * i:4 * i + 4, :])
                B.act(X2[P, 4 * i:4 * i + 4, :], pb.rearrange("p (a b) -> p a b", a=4), AF.Sqrt, bias=B.epsc[P, 3:4])
            B.recip(X2[P], X2[P])
            B.tt('dve', kkn[P], kkn[P], X2[P], ALU.mult)
        st.append(kk1)

        def kmod():
            B.ts('dve', X1[P], aa[P], -1.0, ALU.add)
            B.tt('pool', X1[P], X1[P], bc8(kac), ALU.mult)
            B.stt(k_, X1[P], 1.0, k_, ALU.add, ALU.mult)
        st.append(kmod)

        def dec0():
            for h in range(8):
                B.scan(cs[P, h, :], rmask[P, 0:GR], sig[P, h, :], 0.0, ALU.mult, ALU.add)
            B.act(gam[P], cs[P], AF.Exp, scale=-E05)
            B.act(X2[P], cs[P], AF.Exp, scale=E05)
            B.tt('pool', X1[P], cs[P], sig[P], ALU.subtract)
            B.act(X1[P], X1[P], AF.Exp, scale=-E05)
        st.append(dec0)

        def dec1():
            B.stt(At[P], X1[P], -1.0, kkn[P], ALU.mult, ALU.mult)
            B.tt('pool', X1[P], kkn[P], aa[P], ALU.mult)
            B.tt('dve', Bt[P], X1[P], X2[P], ALU.mult)
            B.tt('dve', Kt[P], k_, X2[P], ALU.mult)
            B.tt('pool', Rt[P], r_, gam[P], ALU.mult)
        st.append(dec1)
        return st

    def chunk_stages(g, c, M):
        G = DB[g % 2]
        psh, At, Bt, Kt, Rt, gam, yT = G["psh"], G["At"], G["Bt"], G["Kt"], G["Rt"], G["gam"], G["yT"]
        cc = slice(c * 64, (c + 1) * 64)
        st = []
        for nm, X, Y, mi in (("NbaT", At, Bt, 2), ("Nba", Bt, At, 0), ("Nka", Kt, At, 0), ("Mbr", Bt, Rt, 1),
                             ("Mkr", Kt, Rt, 1)):
            st.append(lambda nm=nm, X=X, Y=Y, mi=mi: mm8(lambda h: X[P, h, cc], lambda h: Y[P, h, cc],
                                                         lambda pv: B.tt('dve', M[nm][P], pv, mbc(mi), ALU.mult)))
        for nm, X in (("Btok", Bt), ("Ktok", Kt)):
            st.append(lambda nm=nm, X=X: mm8(lambda h: X[P, h, cc], lambda h: identb[P, :], cp(M[nm][P])))

        def vtok():
            pb = nb()
            for h in range(8):
                B.tr(pb[:, h * 64:(h + 1) * 64], psh[P, 16 + h, cc], B.ident[P, 0:64])
            B.copy(ev_eng(), M["Vtok"][P], bv(pb))
        st.append(vtok)

        def tinit():
            B.tt('pool', M["Ta"][P], M["Nba"][P], mbc(3), ALU.add)
            B.tt('pool', M["TTa"][P], M["NbaT"][P], mbc(3), ALU.add)
        st.append(tinit)
        cur = [M["Nba"], M["NbaT"], M["Ta"], M["TTa"]]
        sets = [(M["Pa"], M["PTa"], M["Tb"], M["TTb"]), (M["Pb"], M["PTb"], M["Ta"], M["TTa"])]
        for s_ in range(5):
            Pn, PTn, Tn, TTn = sets[s_ % 2]
            Pc, PTc, Tc, TTc = cur
            last = (s_ == 4)

            def sq_stage(Pc=Pc, PTc=PTc, Pn=Pn, PTn=PTn, last=last):
                mm8(lambda h: PTc[P, h, :], lambda h: Pc[P, h, :], lambda pv: B.copy('act', Pn[P], pv))
                if not last:
                    mm8(lambda h: Pc[P, h, :], lambda h: PTc[P, h, :], lambda pv: B.copy('dve', PTn[P], pv))

            def t_stage(Pn=Pn, Tc=Tc, TTc=TTc, Tn=Tn, TTn=TTn, last=last):
                mm8(lambda h: TTc[P, h, :], lambda h: Pn[P, h, :], lambda pv: B.tt('dve', Tn[P], pv, Tc[P], ALU.add))
                if not last:
                    mm8(lambda h: Pn[P, h, :], lambda h: TTc[P, h, :], lambda pv: B.tt('dve', TTn[P], pv, TTc[P], ALU.add))
            st.append(sq_stage)
            st.append(t_stage)
            cur = [Pn, PTn, Tn, TTn]
        Tfin = cur[2]
        st.append(lambda: mm8(lambda h: M["Nka"][P, h, :], lambda h: M["Vtok"][P, h, :], cp(M["X0v"][P])))
        st.append(lambda: mm8(lambda h: M["Vtok"][P, h, :], lambda h: M["Mkr"][P, h, :], cp(M["Yv"][P])))
        st.append(lambda: mm8(lambda h: M["Ktok"][P, h, :], lambda h: M["Vtok"][P, h, :], cp(M["KV"][P])))

        def seq1():
            B.tt('pool', M["KVS"][P], ST[P], M["KV"][P], ALU.add)
            mm8(lambda h: At[P, h, cc], lambda h: STb[P, h, :], lambda pv: B.tt('dve', M["X0"][P], pv, M["X0v"][P], ALU.add))

        def seq2():
            mm8(lambda h: Tfin[P, h, :], lambda h: M["X0"][P, h, :], lambda pv: B.copy('act', M["U"][P], pv))

        def seq3():
            pb = nb()
            for h in range(8):
                B.mm(pb[:, h * 64:(h + 1) * 64], M["Btok"][P, h, :], M["U"][P, h, :])
            B.tt('dve', M["Sf"][P], bv(pb), M["KVS"][P], ALU.add)
            pb2 = nb()
            for h in range(8):
                B.mm(pb2[:, h * 64:(h + 1) * 64], STb[P, h, :], Rt[P, h, cc], True, False)
                B.mm(pb2[:, h * 64:(h + 1) * 64], M["U"][P, h, :], M["Mbr"][P, h, :], False, True)
            gbc = gam[P, :, c * 64 + 63:c * 64 + 64].to_broadcast([64, 8, 64])
            B.tt('dve', ST[P], M["Sf"][P], gbc, ALU.mult)
            B.copy('act', STb[P], ST[P])
            B.tt('dve', yT[P, :, cc], bv(pb2), M["Yv"][P], ALU.add)
        return st, [seq1, seq2, seq3]

    def tail_steps(g):
        G = DB[g % 2]
        psh, gt, yT, yo = G["psh"], G["gt"], G["yT"], G["yo"]
        g0 = g * GR
        r_ = psh[P, 0:8, :]
        k_ = psh[P, 8:16, :]
        st = []

        def t0():
            for i in range(2):
                hs = slice(4 * i, 4 * i + 4)
                pb = nb()
                B.mm(pb, onesm[P, 0:64], yT[P, hs, :])
                B.tt('dve', T1[P, hs, :], yT[P, hs, :], pb.rearrange("p (a b) -> p a b", a=4), ALU.subtract)
            B.tt('pool', T2[P], T1[P], T1[P], ALU.mult)
        st.append(t0)

        def t1():
            for i in range(2):
                hs = slice(4 * i, 4 * i + 4)
                pb = nb()
                B.mm(pb, onesm[P, 0:64], T2[P, hs, :])
                B.act(T2[P, hs, :], pb.rearrange("p (a b) -> p a b", a=4), AF.Sqrt, bias=B.epsc[P, 1:2])
            B.recip(T2[P], T2[P])
            B.tt('dve', T1[P], T1[P], T2[P], ALU.mult)
        st.append(t1)

        def t2():
            for h in range(8):
                B.ts('dve', T1[P, h, :], T1[P, h, :], lng[P, h:h + 1], ALU.mult, lnb[P, h:h + 1], ALU.add)
            B.tt('pool', T2[P], r_, k_, ALU.mult)
            B.tt('pool', T2[P], T2[P], bc8(rkc), ALU.mult)
        st.append(t2)

        def t3():
            for i in range(2):
                hs = slice(4 * i, 4 * i + 4)
                pb = nb()
                B.mm(pb, B.ones[P, 0:64], T2[P, hs, :])
                B.tt('dve', T2[P, hs, :], pb.rearrange("p (a b) -> p a b", a=4), psh[P, 16 + 4 * i:20 + 4 * i, :], ALU.mult)
            B.tt('pool', T1[P], T1[P], T2[P], ALU.add)
            B.tt('dve', yo[P], T1[P], gt[P], ALU.mult)
            B.store('sp', yc_v[:, :, g0:g0 + GR], yo[P], ("ycT", g0, g0 + GR))
        st.append(t3)
        return st

    for f in prep_steps(0):
        f()
    for g in range(NGR):
        sa, seqa = chunk_stages(g, 0, MM[0])
        sb_, seqb = chunk_stages(g, 1, MM[1])
        main = []
        for fa, fb in zip(sa, sb_):
            main.append((fa, fb))
        for f in seqa:
            main.append((f,))
        for f in seqb:
            main.append((f,))
        side = []
        if g > 0:
            side += tail_steps(g - 1)
        if g + 1 < NGR:
            side += prep_steps(g + 1)
        ns, nm_ = len(side), len(main)
        si = 0
        for i, fs in enumerate(main):
            for f in fs:
                f()
            tgt = (i + 1) * ns // nm_
            while si < tgt:
                side[si]()
                si += 1
    for f in tail_steps(NGR - 1):
        f()


def make_consts():
    c = {}
    c["c_ident"] = np.eye(128, dtype=np.float32)
    m = np.zeros((64, 4, 64), np.float32)
    r = np.arange(64)[:, None]
    cc = np.arange(64)[None, :]
    m[:, 0, :] = (r < cc)
    m[:, 1, :] = (r <= cc)
    m[:, 2, :] = (r > cc)
    m[:, 3, :] = (r == cc)
    c["c_m64"] = m
    am = np.zeros((128, 4, 512), np.float32)
    k = np.arange(128)[:, None]
    q = np.arange(512)[None, :]
    for j in range(4):
        am[:, j, :] = (q >= j * 128 + k)
    c["c_amask"] = am
    t = np.arange(128)[:, None]
    s = np.arange(128)[None, :]
    c["c_tril"] = (t >= s).astype(np.float32)
    rope = np.zeros((96, 4), np.float32)
    inv = (10000.0 ** (-np.arange(0, 32, 2, dtype=np.float32) / 32)).astype(np.float32)
    rope[64:80, 0] = inv
    rope[80:96, 0] = inv
    rope[64:80, 1] = -1.0
    rope[80:96, 1] = 1.0
    c["c_rope"] = rope
    es = np.zeros((32, 2, 96), np.float32)
    for i in range(32):
        es[i, 0, 64 + i] = 1.0
        es[i, 1, 64 + (i + 16) % 32] = 1.0
    c["c_esel"] = es
    rs = np.ones((64, 256), np.float32)
    rs[:, ::64] = 0.0
    c["c_reset"] = rs
    return c


_CACHE = {}


def kernel(**inputs):
    T = 4096
    if "nc" not in _CACHE:
        _CACHE["nc"] = build(T).nc
    nc = _CACHE["nc"]
    consts = make_consts()
    x = np.ascontiguousarray(inputs["x"], dtype=np.float32)
    pos = np.ascontiguousarray(inputs["positions"], dtype=np.int32)
    in_maps = []
    active = {0: 0, 1: 1, 4: 2, 5: 3}
    wts = {k: np.ascontiguousarray(v, dtype=np.float32) for k, v in inputs.items() if k not in ("x", "positions")}
    zw = {k: np.zeros_like(v) for k, v in wts.items()}
    for c in range(8):
        if c in active:
            b = active[c]
            m = {"x": x[b], "positions": pos[b:b + 1]}
            m.update(wts)
        else:
            m = {"x": np.zeros_like(x[0]), "positions": np.zeros_like(pos[0:1])}
            m.update(zw)
        m.update(consts)
        in_maps.append(m)
    res = run_bass_kernel_spmd(nc, in_maps, core_ids=list(range(8)))
    out = np.stack([res.results[c]["out"] for c in (0, 1, 4, 5)], axis=0)
    return out.astype(np.float32)
```
